# Optimizing a Trainium2 kernel written in Bass

```python
import math
import jax
import jax.numpy as jnp
from jax import lax
import numpy as np

D_MODEL = 2048
BATCH = 2
SEQ = 16384
DEPTH = 1

CTX_LEN = 256
GRID_W = 64
RMS_EPS = 1e-6

A_WIDTH = D_MODEL // 2
A_HEAD = 64
A_HEADS = A_WIDTH // A_HEAD
A_DECAY_LORA = 96
A_ICLR_LORA = 128
A_GATE_LORA = 256
A_GN_EPS = 64e-5

B_WIDTH = D_MODEL // 2
B_HEAD = 128
B_HEADS = B_WIDTH // B_HEAD
B_CONV = 3
B_CHUNK = 64

D_FF = 4 * D_MODEL

A_COLS = 3 * A_WIDTH + A_DECAY_LORA + A_ICLR_LORA + A_GATE_LORA
B_COLS = 4 * B_WIDTH + 4 * B_HEADS
GATE_COLS = 2 * D_MODEL
IN_COLS = A_COLS + B_COLS + GATE_COLS

kernel_name = 'hybrid_rwkv7_gdn_dit_block'


def split_cols(z, sizes):
    offs = np.cumsum(sizes)[:-1].tolist()
    return jnp.split(z, offs, axis=-1)


def rms_norm(x, w):
    xf = x.astype(jnp.float32)
    y = xf * lax.rsqrt(jnp.mean(xf * xf, axis=-1, keepdims=True) + RMS_EPS)
    return (y * w.astype(jnp.float32)).astype(x.dtype)


def l2_normalize(z):
    return z * lax.rsqrt(jnp.sum(z * z, axis=-1, keepdims=True) + 1e-12)


def modulate(h, shift, scale):
    return h * (1.0 + scale) + shift


def grid_shift(z):
    bsz, t, c = z.shape
    g = z.reshape(bsz, t // GRID_W, GRID_W, c // 4, 4)
    zero_col = jnp.zeros_like(g[:, :, :1, :, 0])
    zero_row = jnp.zeros_like(g[:, :1, :, :, 0])
    left = jnp.concatenate([zero_col, g[:, :, :-1, :, 0]], axis=2)
    right = jnp.concatenate([g[:, :, 1:, :, 1], zero_col], axis=2)
    up = jnp.concatenate([zero_row, g[:, :-1, :, :, 2]], axis=1)
    down = jnp.concatenate([g[:, 1:, :, :, 3], zero_row], axis=1)
    return jnp.stack([left, right, up, down], axis=-1).reshape(bsz, t, c)


def seq_shift(z):
    bsz, t, c = z.shape
    g = z.reshape(bsz, t, c // 2, 2)
    zero = jnp.zeros_like(g[:, :1, :, 0])
    prev = jnp.concatenate([zero, g[:, :-1, :, 0]], axis=1)
    nxt = jnp.concatenate([g[:, 1:, :, 1], zero], axis=1)
    return jnp.stack([prev, nxt], axis=-1).reshape(bsz, t, c)


def centred_conv(z, w):
    k = w.shape[0]
    return lax.conv_general_dilated(
        z, w.astype(z.dtype)[:, None, :], window_strides=(1,),
        padding=[((k - 1) // 2, k // 2)], dimension_numbers=('NWC', 'WIO', 'NWC'),
        feature_group_count=z.shape[-1])


def rwkv7_prepare(z, w0, w_up, a0, a_up, k_k, k_a):
    bsz, t, _ = z.shape
    heads = lambda u: u.reshape(bsz, t, A_HEADS, A_HEAD)
    r, k, v, xw, xa, xg = split_cols(z, [A_WIDTH] * 3 + [A_DECAY_LORA, A_ICLR_LORA, A_GATE_LORA])
    kk = l2_normalize(heads(k * k_k))
    tw = jnp.tanh(xw)
    dirs = []
    for d in range(2):
        w = -jax.nn.softplus(-(w0[d] + tw @ w_up[d])) - 0.5
        a = jax.nn.sigmoid(a0[d] + xa @ a_up[d])
        k_d = k * (1.0 + (a - 1.0) * k_a)
        dirs.append((heads(jnp.exp(-jnp.exp(w))), heads(k_d), heads(a) * kk))
    return heads(r), heads(v), kk, xg, dirs


def rwkv7_scan(r, decay, k, v, kk, b, s0, reverse, with_out):
    seq = (decay, k, v, kk, b) + ((r,) if with_out else ())
    xs = tuple(jnp.swapaxes(u, 0, 1) for u in seq)

    def step(s, inp):
        dec_t, k_t, v_t, kk_t, b_t = inp[:5]
        sa = jnp.einsum('bhvk,bhk->bhv', s, kk_t)
        s = (s * dec_t[:, :, None, :] - sa[..., None] * b_t[:, :, None, :]
             + v_t[..., None] * k_t[:, :, None, :])
        if with_out:
            return s, jnp.einsum('bhvk,bhk->bhv', s, inp[5])
        return s, None

    s, ys = lax.scan(step, s0, xs, reverse=reverse)
    return (jnp.swapaxes(ys, 0, 1) if with_out else None), s


def rwkv7_readout(y, r, k_dirs, v, xg, g_up, r_k, ln_w, ln_b):
    bsz, t = y.shape[:2]
    mean = jnp.mean(y, axis=-1, keepdims=True)
    var = jnp.mean(jnp.square(y - mean), axis=-1, keepdims=True)
    yn = ((y - mean) * lax.rsqrt(var + A_GN_EPS)).reshape(bsz, t, A_WIDTH) * ln_w + ln_b
    bonus = sum(jnp.sum(r * k_d * r_k, axis=-1, keepdims=True) for k_d in k_dirs) * v
    gate = jax.nn.sigmoid(xg) @ g_up
    return (yn + bonus.reshape(bsz, t, A_WIDTH)) * gate


def rwkv7_branch(za, za_ctx, mu, w0, w_up, a0, a_up, g_up, k_k, k_a, r_k, ln_w, ln_b, ctx_out):
    za = za + (grid_shift(za) - za) * mu
    za_ctx = za_ctx + (seq_shift(za_ctx) - za_ctx) * mu
    r, v, kk, xg, dirs = rwkv7_prepare(za, w0, w_up, a0, a_up, k_k, k_a)
    rc, vc, kkc, xgc, dirs_c = rwkv7_prepare(za_ctx, w0, w_up, a0, a_up, k_k, k_a)
    s0 = jnp.zeros((za.shape[0], A_HEADS, A_HEAD, A_HEAD), jnp.float32)
    ys, ys_c = [], []
    for d in range(2):
        rev = d == 1
        dec_c, k_c, b_c = dirs_c[d]
        y_c, s_ctx = rwkv7_scan(rc, dec_c, k_c, vc, kkc, b_c, s0, rev, ctx_out)
        dec, k_d, b = dirs[d]
        y_d, _ = rwkv7_scan(r, dec, k_d, v, kk, b, s_ctx, rev, True)
        ys.append(y_d)
        ys_c.append(y_c)
    out = rwkv7_readout(ys[0] + ys[1], r, [dirs[0][1], dirs[1][1]], v, xg, g_up, r_k, ln_w, ln_b)
    out_c = None
    if ctx_out:
        out_c = rwkv7_readout(ys_c[0] + ys_c[1], rc, [dirs_c[0][1], dirs_c[1][1]], vc, xgc,
                              g_up, r_k, ln_w, ln_b)
    return out, out_c


def gdn_prepare(z, conv_w, a_log, dt_bias):
    bsz, t, _ = z.shape
    qkv, gz, a_cols, b_cols = split_cols(z, [3 * B_WIDTH, B_WIDTH, 2 * B_HEADS, 2 * B_HEADS])
    qkv = jax.nn.silu(centred_conv(qkv, conv_w))
    heads = lambda u: u.reshape(bsz, t, B_HEADS, B_HEAD)
    q, k, v = (heads(u) for u in jnp.split(qkv, 3, axis=-1))
    q = l2_normalize(q) * (B_HEAD ** -0.5)
    k = l2_normalize(k)
    g = -jnp.exp(a_log) * jax.nn.softplus(a_cols.reshape(bsz, t, 2, B_HEADS) + dt_bias)
    beta = jax.nn.sigmoid(b_cols.reshape(bsz, t, 2, B_HEADS))
    return q, k, v, g, beta, gz


def gdn_chunked(q, k, v, g, beta, s0, with_out):
    bsz, t, nh, _ = q.shape
    dv = v.shape[-1]
    nc = t // B_CHUNK
    blk = lambda u: jnp.swapaxes(u.reshape((bsz, nc, B_CHUNK) + u.shape[2:]), 2, 3)
    q, k, v, g, beta = blk(q), blk(k), blk(v), blk(g), blk(beta)
    gc = jnp.cumsum(g, axis=-1)
    idx = jnp.arange(B_CHUNK)
    incl = idx[:, None] >= idx[None, :]
    decay = jnp.exp(jnp.where(incl, gc[..., :, None] - gc[..., None, :], -jnp.inf))
    kb = k * beta[..., None]
    a_mat = jnp.einsum('bnhid,bnhjd->bnhij', kb, k) * decay
    a_mat = jnp.where(idx[:, None] > idx[None, :], a_mat, 0.0) + jnp.eye(B_CHUNK, dtype=jnp.float32)
    rhs = jnp.concatenate([v * beta[..., None], kb * jnp.exp(gc)[..., None]], axis=-1)
    sol = lax.linalg.triangular_solve(a_mat, rhs, left_side=True, lower=True, unit_diagonal=True)
    u, w = sol[..., :dv], sol[..., dv:]
    k_end = k * jnp.exp(gc[..., -1:] - gc)[..., None]
    g_end = jnp.exp(gc[..., -1])
    seq = (u, w, k_end, g_end)
    if with_out:
        attn = jnp.einsum('bnhid,bnhjd->bnhij', q, k) * decay
        seq = seq + (attn, q * jnp.exp(gc)[..., None])
    xs = tuple(jnp.moveaxis(z, 1, 0) for z in seq)

    def step(s, inp):
        u_n, w_n, k_n, g_n = inp[:4]
        v_new = u_n - jnp.einsum('bhcd,bhdv->bhcv', w_n, s)
        s_new = s * g_n[..., None, None] + jnp.einsum('bhcd,bhcv->bhdv', k_n, v_new)
        if with_out:
            attn_n, qg_n = inp[4:]
            o = jnp.einsum('bhcd,bhdv->bhcv', qg_n, s) + jnp.einsum('bhij,bhjv->bhiv', attn_n, v_new)
            return s_new, o
        return s_new, None

    s, o = lax.scan(step, s0, xs)
    if with_out:
        o = jnp.swapaxes(jnp.moveaxis(o, 0, 1), 2, 3).reshape(bsz, t, nh, dv)
    return o, s


def gdn_readout(o, gz, norm_w):
    bsz, t = o.shape[:2]
    on = o * lax.rsqrt(jnp.mean(o * o, axis=-1, keepdims=True) + RMS_EPS) * norm_w
    return (on * jax.nn.silu(gz.reshape(bsz, t, B_HEADS, B_HEAD))).reshape(bsz, t, B_WIDTH)


def gdn_branch(zb, zb_ctx, conv_w, a_log, dt_bias, norm_w, ctx_out):
    q, k, v, g, beta, gz = gdn_prepare(zb, conv_w, a_log, dt_bias)
    qc, kc, vc, gcx, betac, gzc = gdn_prepare(zb_ctx, conv_w, a_log, dt_bias)
    s0 = jnp.zeros((zb.shape[0], B_HEADS, B_HEAD, B_HEAD), jnp.float32)
    outs, outs_c = [], []
    for d in range(2):
        fl = (lambda u: u[:, ::-1]) if d == 1 else (lambda u: u)
        o_c, s_ctx = gdn_chunked(fl(qc), fl(kc), fl(vc), fl(gcx[:, :, d]), fl(betac[:, :, d]), s0, ctx_out)
        o, _ = gdn_chunked(fl(q), fl(k), fl(v), fl(g[:, :, d]), fl(beta[:, :, d]), s_ctx, True)
        outs.append(fl(o))
        if ctx_out:
            outs_c.append(fl(o_c))
    out = gdn_readout(outs[0] + outs[1], gz, norm_w)
    out_c = gdn_readout(outs_c[0] + outs_c[1], gzc, norm_w) if ctx_out else None
    return out, out_c


def merge_branches(ya, yb, pg, up_a, up_b, out_w):
    gate_a, gate_b = jnp.split(jax.nn.sigmoid(pg), 2, axis=-1)
    return (gate_a * (ya @ up_a) + gate_b * (yb @ up_b)) @ out_w


def squared_relu_mlp(h, w1, w2):
    return jnp.square(jax.nn.relu(h @ w1)) @ w2


def trunk_layer(x, ctx, c, c_ctx, mod_w, mod_b, norm1_w, norm2_w, in_w, rwkv_mu, rwkv_w0,
                rwkv_w_up, rwkv_a0, rwkv_a_up, rwkv_g_up, rwkv_k_k, rwkv_k_a, rwkv_r_k, rwkv_ln_w,
                rwkv_ln_b, gdn_conv_w, gdn_a_log, gdn_dt_bias, gdn_norm_w, up_a, up_b, out_w,
                mlp_w1, mlp_w2, last):
    dt = x.dtype
    f32 = jnp.float32
    mod = jax.nn.silu(c) @ mod_w + mod_b
    mod_c = jax.nn.silu(c_ctx) @ mod_w + mod_b
    sh1, sc1, g1, sh2, sc2, g2 = jnp.split(mod[:, None, :], 6, axis=-1)
    csh1, csc1, cg1, csh2, csc2, cg2 = jnp.split(mod_c[None, None, :], 6, axis=-1)

    h = modulate(rms_norm(x, norm1_w), sh1, sc1)
    hc = modulate(rms_norm(ctx, norm1_w), csh1, csc1)
    p = h @ in_w
    pc = hc @ (in_w[:, :A_COLS + B_COLS] if last else in_w)
    pa, pb, pg = split_cols(p, [A_COLS, B_COLS, GATE_COLS])
    pca, pcb = pc[..., :A_COLS], pc[..., A_COLS:A_COLS + B_COLS]

    ya, ya_c = rwkv7_branch(pa.astype(f32), pca.astype(f32), rwkv_mu, rwkv_w0, rwkv_w_up, rwkv_a0,
                            rwkv_a_up, rwkv_g_up, rwkv_k_k, rwkv_k_a, rwkv_r_k, rwkv_ln_w,
                            rwkv_ln_b, not last)
    yb, yb_c = gdn_branch(pb.astype(f32), pcb.astype(f32), gdn_conv_w, gdn_a_log, gdn_dt_bias,
                          gdn_norm_w, not last)

    x = x + g1 * merge_branches(ya.astype(dt), yb.astype(dt), pg, up_a, up_b, out_w)
    x = x + g2 * squared_relu_mlp(modulate(rms_norm(x, norm2_w), sh2, sc2), mlp_w1, mlp_w2)
    if not last:
        ctx = ctx + cg1 * merge_branches(ya_c.astype(dt), yb_c.astype(dt), pc[..., A_COLS + B_COLS:],
                                         up_a, up_b, out_w)
        ctx = ctx + cg2 * squared_relu_mlp(modulate(rms_norm(ctx, norm2_w), csh2, csc2), mlp_w1, mlp_w2)
    return x, ctx


def setup_inputs(seed: int = 0) -> dict:
    key = jax.random.key(seed)
    ks = iter(jax.random.split(key, 40))
    nrm = lambda shape, scale: jax.random.normal(next(ks), shape, jnp.float32) * scale
    uni = lambda shape, lo, hi: jax.random.uniform(next(ks), shape, jnp.float32, lo, hi)
    L, D = DEPTH, D_MODEL
    dt_init = jnp.exp(uni((L, 2, B_HEADS), math.log(1e-3), math.log(1e-1)))
    return {
        'x': nrm((BATCH, SEQ, D), 1.0),
        'c': nrm((BATCH, D), 1.0),
        'ctx': nrm((BATCH, CTX_LEN, D), 1.0),
        'c_ctx': nrm((D,), 1.0),
        'mod_w': nrm((L, D, 6 * D), 0.5 * D ** -0.5),
        'mod_b': nrm((L, 6 * D), 0.02),
        'norm1_w': 1.0 + nrm((L, D), 0.1),
        'norm2_w': 1.0 + nrm((L, D), 0.1),
        'in_w': nrm((L, D, IN_COLS), D ** -0.5),
        'rwkv_mu': uni((L, A_COLS), 0.0, 1.0),
        'rwkv_w0': uni((L, 2, A_WIDTH), -3.0, 1.0),
        'rwkv_w_up': nrm((L, 2, A_DECAY_LORA, A_WIDTH), 0.5 * A_DECAY_LORA ** -0.5),
        'rwkv_a0': nrm((L, 2, A_WIDTH), 0.5),
        'rwkv_a_up': nrm((L, 2, A_ICLR_LORA, A_WIDTH), 0.5 * A_ICLR_LORA ** -0.5),
        'rwkv_g_up': nrm((L, A_GATE_LORA, A_WIDTH), A_GATE_LORA ** -0.5),
        'rwkv_k_k': 0.85 + nrm((L, A_WIDTH), 0.05),
        'rwkv_k_a': 1.0 + nrm((L, A_WIDTH), 0.05),
        'rwkv_r_k': nrm((L, A_HEADS, A_HEAD), 0.1),
        'rwkv_ln_w': 1.0 + nrm((L, A_WIDTH), 0.1),
        'rwkv_ln_b': nrm((L, A_WIDTH), 0.02),
        'gdn_conv_w': nrm((L, B_CONV, 3 * B_WIDTH), B_CONV ** -0.5),
        'gdn_a_log': jnp.log(uni((L, 2, B_HEADS), 1.0, 16.0)),
        'gdn_dt_bias': dt_init + jnp.log(-jnp.expm1(-dt_init)),
        'gdn_norm_w': 1.0 + nrm((L, B_HEAD), 0.1),
        'up_a': nrm((L, A_WIDTH, D), A_WIDTH ** -0.5),
        'up_b': nrm((L, B_WIDTH, D), B_WIDTH ** -0.5),
        'out_w': nrm((L, D, D), D ** -0.5),
        'mlp_w1': nrm((L, D, D_FF), D ** -0.5),
        'mlp_w2': nrm((L, D_FF, D), D_FF ** -0.5),
        'final_norm_w': 1.0 + nrm((D,), 0.1),
    }


def reference(x, c, ctx, c_ctx, mod_w, mod_b, norm1_w, norm2_w, in_w, rwkv_mu, rwkv_w0, rwkv_w_up,
              rwkv_a0, rwkv_a_up, rwkv_g_up, rwkv_k_k, rwkv_k_a, rwkv_r_k, rwkv_ln_w, rwkv_ln_b,
              gdn_conv_w, gdn_a_log, gdn_dt_bias, gdn_norm_w, up_a, up_b, out_w, mlp_w1, mlp_w2,
              final_norm_w):
    for l in range(DEPTH):
        x, ctx = trunk_layer(
            x, ctx, c, c_ctx, mod_w[l], mod_b[l], norm1_w[l], norm2_w[l], in_w[l], rwkv_mu[l],
            rwkv_w0[l], rwkv_w_up[l], rwkv_a0[l], rwkv_a_up[l], rwkv_g_up[l], rwkv_k_k[l],
            rwkv_k_a[l], rwkv_r_k[l], rwkv_ln_w[l], rwkv_ln_b[l], gdn_conv_w[l], gdn_a_log[l],
            gdn_dt_bias[l], gdn_norm_w[l], up_a[l], up_b[l], out_w[l], mlp_w1[l], mlp_w2[l],
            last=(l == DEPTH - 1))
    return rms_norm(x, final_norm_w)
```

```python
import numpy as np
import concourse.bass as bass
import concourse.mybir as mybir

F32 = mybir.dt.float32
BF16 = mybir.dt.bfloat16
ALU = mybir.AluOpType
AF = mybir.ActivationFunctionType
AX = mybir.AxisListType

COMPUTE = ("pe", "act", "dve", "pool")
ALLENG = ("pe", "act", "dve", "pool", "sp")


class T:
    def __init__(self, name, h, space):
        self.name = name
        self.h = h
        self.space = space
        self.lw = None
        self.rd = []
        self.dq = None
        self.lw_is_dma = False

    def __getitem__(self, idx):
        return self.h[idx]


class DmaQ:
    def __init__(self, sem):
        self.sem = sem
        self.cnt = 0
        self.sw = False


class Prog:
    def __init__(self, nc, stack):
        self.nc = nc
        self.stack = stack
        self.root = stack
        self.ops = {e: [] for e in ALLENG}
        self.seq = {e: 0 for e in ALLENG}
        self.sem = {}
        self.seen = {e: {} for e in ALLENG}
        self.semobjs = {}
        self.dma_sems = []
        self.free_dq = []
        self.scope_dqs = [[]]
        self.n_sems = 0
        self.epoch = 0
        self.nbar = 0
        for e in ALLENG:
            self.sem[e] = self._newsem(f"s_{e}")
        self.Bp = self._newsem("bar_p")
        self.Bc = self._newsem("bar_c")
        self.ntile = 0

    def _newsem(self, name):
        s = self.root.enter_context(self.nc.semaphore(name))
        self.n_sems += 1
        self.semobjs[id(s)] = s
        return s

    def _newdq(self, name):
        if self.free_dq:
            dq = self.free_dq.pop()
        else:
            dq = DmaQ(self._newsem(f"d_{name}_{self.n_sems}"))
            self.dma_sems.append(dq)
        self.scope_dqs[-1].append(dq)
        return dq

    def sb(self, name, shape, dtype=F32):
        self.ntile += 1
        h = self.stack.enter_context(self.nc.sbuf_tensor(f"{name}_{self.ntile}", list(shape), dtype))
        return T(name, h, "sb")

    def ps(self, name, shape, dtype=F32):
        self.ntile += 1
        h = self.stack.enter_context(self.nc.psum_tensor(f"{name}_{self.ntile}", list(shape), dtype))
        return T(name, h, "ps")

    def dram(self, name, shape, dtype=F32, kind="Internal"):
        h = self.nc.dram_tensor(name, list(shape), dtype, kind=kind).ap()
        return T(name, h, "dram")

    def share(self, tiles, name="shr"):
        dq = self._newdq(name)
        for t in tiles:
            t.dq = dq

    def scope(self):
        import contextlib
        prog = self

        @contextlib.contextmanager
        def cm():
            old = prog.stack
            with contextlib.ExitStack() as ns:
                prog.stack = ns
                prog.scope_dqs.append([])
                try:
                    yield
                finally:
                    prog.barrier()
                    prog.free_dq.extend(prog.scope_dqs.pop())
                    prog.stack = old
        return cm()

    def _deps(self, eng, reads, writes):
        deps = []
        for t in reads:
            if t.lw is not None:
                deps.append(t.lw)
        for t in writes:
            if t.lw is not None:
                deps.append(t.lw)
            deps.extend(t.rd)
        out = []
        seen = self.seen[eng]
        best = {}
        for dep in deps:
            s, v, src_eng = dep[0], dep[1], dep[2]
            if dep[4] != self.epoch:
                continue
            if src_eng == "pe" and eng == "pe":
                continue
            if len(dep) > 3 and dep[3] is not None:
                v = dep[3].cnt
            k = id(s)
            if seen.get(k, 0) >= v:
                continue
            if best.get(k, (None, 0))[1] < v:
                best[k] = (s, v)
        for k, (s, v) in best.items():
            seen[k] = v
            out.append((s, v))
        return out

    def op(self, eng, fn, reads=(), writes=()):
        reads = [t for t in reads if t is not None]
        writes = [t for t in writes if t is not None]
        if fn.__closure__:
            import types
            fn = types.FunctionType(fn.__code__, fn.__globals__, fn.__name__, fn.__defaults__,
                                    tuple(types.CellType(c.cell_contents) for c in fn.__closure__))
        waits = self._deps(eng, reads, writes)
        self.seq[eng] += 1
        me = (self.sem[eng], self.seq[eng], eng, None, self.epoch)
        self.ops[eng].append((waits, fn, (self.sem[eng], 1)))
        for t in reads:
            t.rd.append(me)
        for t in writes:
            t.lw = me
            t.rd = []
            t.lw_is_dma = False
        return me

    def dma(self, q, out_ap, in_ap, reads=(), writes=(), group=False, **kw):
        reads = [t for t in reads if t is not None]
        writes = [t for t in writes if t is not None]
        allt = list(writes) + list(reads)
        carrier = None
        for t in allt:
            if t.space == "sb":
                carrier = t
                break
        if carrier is None:
            carrier = allt[0]
        reads = [t for t in reads if t.space != "dram" or t is carrier]
        writes = [t for t in writes if t.space != "dram" or t is carrier]
        if carrier.dq is None:
            carrier.dq = self._newdq(carrier.name)
        wr = list(writes)
        if group:
            deps_w = []
            for t in wr:
                if not (t.lw_is_dma and t is carrier and t.lw[4] == self.epoch):
                    deps_w.append(t)
                else:
                    pass
            waits = self._deps(q, reads, deps_w)
            extra = []
            for t in wr:
                if t.lw_is_dma and t is carrier and t.lw[4] == self.epoch:
                    extra.extend(t.rd)
            seen = self.seen[q]
            for dep in extra:
                s, v = dep[0], dep[1]
                if dep[4] != self.epoch:
                    continue
                if len(dep) > 3 and dep[3] is not None:
                    v = dep[3].cnt
                if seen.get(id(s), 0) < v:
                    seen[id(s)] = v
                    waits.append((s, v))
        else:
            waits = self._deps(q, reads, wr)
        if q == "pool":
            carrier.dq.sw = True
        carrier.dq.cnt += 16
        me = (carrier.dq.sem, carrier.dq.cnt, "dma", carrier.dq, self.epoch)
        nc = self.nc

        def fn(engobj, out_ap=out_ap, in_ap=in_ap, kw=kw):
            return engobj.dma_start(out=out_ap, in_=in_ap, **kw)
        self.ops[q].append((waits, fn, (carrier.dq.sem, 16)))
        for t in reads:
            t.rd.append(me)
        for t in writes:
            t.lw = me
            t.rd = [] if not (group and t.lw_is_dma and t is carrier and t.lw[4] == self.epoch) else t.rd
            t.lw_is_dma = True
        return me

    def barrier(self):
        targets = []
        for e in ALLENG:
            if self.seq[e] > 0:
                targets.append((self.sem[e], self.seq[e]))
        for dq in self.dma_sems:
            if dq.cnt > 0:
                targets.append((dq.sem, dq.cnt))
        self.nbar += 1
        kbar = self.nbar
        Bp, Bc = self.Bp, self.Bc
        for e in ALLENG:
            self.ops[e].append((list(targets), (lambda eng: (eng.sem_inc(Bp, 1), None)[1]), None))
        import os
        swsems = [dq.sem for dq in self.dma_sems if dq.sw]
        clr = [s_ for (s_, v_) in targets if not any(s_ is x for x in swsems)]
        esems = [self.sem[e] for e in ALLENG]
        if os.environ.get("FW_CLEAR_ONLY") == "eng":
            clr = [s_ for s_ in clr if any(s_ is x for x in esems)]
        if os.environ.get("FW_CLEAR_ONLY") == "dma":
            clr = [s_ for s_ in clr if not any(s_ is x for x in esems)]

        def do_clear(eng, clr=clr):
            import os
            if not os.environ.get("FW_NOCLEAR"):
                for s_ in clr:
                    eng.sem_clear(s_)
            eng.sem_inc(Bc, 1)
            return None
        self.ops["pool"].append(([(Bp, len(ALLENG) * kbar)], do_clear, None))
        for e in ALLENG:
            self.ops[e].append(([(Bc, kbar)], None, None))
        for e in ALLENG:
            self.seq[e] = 0
            self.seen[e] = {}
        for dq in self.dma_sems:
            if not dq.sw:
                dq.cnt = 0
        self.epoch += 1

    def emit(self):
        nc = self.nc
        engobj = {"pe": "tensor", "act": "scalar", "dve": "vector", "pool": "gpsimd", "sp": "sync"}
        with nc.Block() as block:
            def run(e, eng):
                for (waits, fn, inc) in self.ops[e]:
                    for (s, v) in waits:
                        eng.wait_ge(s, v)
                    if fn is not None:
                        ins = fn(eng)
                        if inc is not None:
                            ins.then_inc(inc[0], inc[1])

            @block.tensor
            def _(eng):
                run("pe", eng)

            @block.scalar
            def _(eng):
                run("act", eng)

            @block.vector
            def _(eng):
                run("dve", eng)

            @block.gpsimd
            def _(eng):
                run("pool", eng)

            @block.sync
            def _(eng):
                run("sp", eng)


import numpy as np
from contextlib import ExitStack

D = 2048
KC = 16
TCTX = 256
TSEQ = 16384
TT = TCTX + TSEQ
NCT = 19
EPS = 1e-6

def core_inputs(inp, b, g):
    f = np.float32
    inw = inp['in_w'][0]
    A = 0; B = 3552; G = 7680
    cs = slice(256 * g, 256 * g + 256)
    cols = []
    cols.append(inw[:, A + 0:A + 1024][:, cs])
    cols.append(inw[:, A + 1024:A + 2048][:, cs])
    cols.append(inw[:, A + 2048:A + 3072][:, cs])
    xw = np.zeros((D, 128), f); xw[:, :96] = inw[:, A + 3072:A + 3168]
    cols.append(xw)
    cols.append(inw[:, A + 3168:A + 3296])
    cols.append(inw[:, A + 3296:A + 3552])
    for j in range(4):
        cols.append(inw[:, B + 1024 * j:B + 1024 * (j + 1)][:, cs])
    ab = np.zeros((D, 128), f)
    hs = [2 * g, 2 * g + 1]
    k = 0
    for base in (B + 4096, B + 4112):
        for d in range(2):
            for h in hs:
                ab[:, k] = inw[:, base + d * 8 + h]; k += 1
    cols.append(ab)
    w_in = np.ascontiguousarray(np.concatenate(cols, axis=1))
    m = {}
    m['w_in'] = w_in
    m['xcat'] = np.ascontiguousarray(np.concatenate([inp['ctx'][b], inp['x'][b]], axis=0))
    cc = np.stack([inp['c'][b], inp['c_ctx']], axis=1)
    m['cT'] = np.ascontiguousarray(cc.reshape(KC, 128, 2).transpose(1, 0, 2))
    m['mod_w'] = inp['mod_w'][0]
    m['mod_bT'] = np.ascontiguousarray(inp['mod_b'][0].reshape(96, 128).T)
    colT = lambda v: np.ascontiguousarray(v.reshape(-1, 128).T)
    m['n1T'] = colT(inp['norm1_w'][0]); m['n2T'] = colT(inp['norm2_w'][0]); m['nfT'] = colT(inp['final_norm_w'])
    m['ident'] = np.eye(128, dtype=f)
    return m


def declare_inputs(nc, m):
    aps = {}
    for k, v in m.items():
        aps[k] = nc.dram_tensor(k, list(v.shape), F32, kind="ExternalInput").ap()
    return aps


class K:
    pass


def build_common(P, I, st):
    k = K()
    nc = P.nc
    k.ident = P.sb("ident", [128, 128])
    P.dma("sp", k.ident[:], I['ident'], writes=[k.ident])
    k.identb = P.sb("identb", [128, 128], BF16)
    P.op("dve", lambda e: e.tensor_copy(k.identb[:], k.ident[:]), reads=[k.ident], writes=[k.identb])
    k.epsc = P.sb("epsc", [128, 1])
    P.op("pool", lambda e: e.memset(k.epsc[:], EPS), writes=[k.epsc])
    k.onesb = P.sb("onesb", [128, 128], BF16)
    P.op("pool", lambda e: e.memset(k.onesb[:], 1.0), writes=[k.onesb])
    cT = P.sb("cT", [128, KC, 2]); P.dma("sp", cT[:], I['cT'], writes=[cT])
    sT = P.sb("sT", [128, KC, 2], BF16)
    P.op("act", lambda e: e.activation(sT[:], cT[:], AF.Silu), reads=[cT], writes=[sT])
    mb = P.sb("mb", [128, 96]); P.dma("sp", mb[:], I['mod_bT'], writes=[mb])
    k.modT = P.sb("modT", [128, 96, 2])
    mwv = I['mod_w'].rearrange("(c p) n -> p c n", p=128)
    sc0 = P.scope(); sc0.__enter__()
    wbuf = [P.sb("modw", [128, KC, 1024], BF16) for _ in range(2)]
    mps = [P.ps("mps", [128, 8, 2]) for _ in range(2)]
    for ch in range(12):
        wb = wbuf[ch % 2]
        for kc in range(KC):
            P.dma("pool", wb[:, kc, :], mwv[:, kc, ch * 1024:(ch + 1) * 1024], writes=[wb], group=True)
        pt = mps[ch % 2]
        for j in range(8):
            for kc in range(KC):
                P.op("pe", lambda e, j=j, kc=kc, wb=wb, pt=pt: e.matmul(pt[:, j, :], wb[:, kc, j * 128:(j + 1) * 128], sT[:, kc, :],
                                                                 start=(kc == 0), stop=(kc == KC - 1)),
                     reads=[wb, sT], writes=[pt])
        for j in range(8):
            jj = ch * 8 + j
            P.op("dve", lambda e, j=j, jj=jj, pt=pt: e.tensor_scalar(k.modT[:, jj, :], pt[:, j, :], mb[:, jj:jj + 1], None, ALU.add),
                 reads=[pt, mb], writes=[k.modT])
    sc0.__exit__(None, None, None)
    n1 = P.sb("n1", [128, KC]); P.dma("sp", n1[:], I['n1T'], writes=[n1])
    n2 = P.sb("n2", [128, KC]); P.dma("sp", n2[:], I['n2T'], writes=[n2])
    k.nf = P.sb("nf", [128, KC]); P.dma("sp", k.nf[:], I['nfT'], writes=[k.nf])

    def scale_vec(name, nw, sc_tile0, col):
        t = P.sb(name, [128, KC])
        P.op("dve", lambda e: e.scalar_tensor_tensor(t[:], k.modT[:, sc_tile0:sc_tile0 + KC, col], 1.0, nw[:], ALU.add, ALU.mult),
             reads=[k.modT, nw], writes=[t])
        return t

    def copy_vec(name, tile0, col):
        t = P.sb(name, [128, KC])
        P.op("dve", lambda e: e.tensor_copy(t[:], k.modT[:, tile0:tile0 + KC, col]), reads=[k.modT], writes=[t])
        return t
    k.s1 = scale_vec("s1", n1, 16, 0); k.sh1 = copy_vec("sh1", 0, 0)
    k.cs1 = scale_vec("cs1", n1, 16, 1); k.csh1 = copy_vec("csh1", 0, 1)
    k.g1 = copy_vec("g1", 32, 0)
    k.s2 = scale_vec("s2", n2, 64, 0); k.sh2 = copy_vec("sh2", 48, 0)
    k.g2 = copy_vec("g2", 80, 0)
    return k


def norm_fm(P, k, xT, ntok, svec, shvec, hT, pools):
    sq, ssps, rstd, tmp = pools['sq'], pools['ssps'], pools['rstd'], pools['tmp']
    for kc in range(KC):
        s = sq[kc % 2]
        P.op("act", lambda e, kc=kc, s=s: e.activation(s[:, :ntok], xT[:, kc, :ntok], AF.Square), reads=[xT], writes=[s])
        P.op("pe", lambda e, kc=kc, s=s: e.matmul(ssps[:, :ntok], k.onesb[:], s[:, :ntok], start=(kc == 0), stop=(kc == KC - 1)),
             reads=[k.onesb, s], writes=[ssps])
    P.op("act", lambda e: e.activation(rstd[:, :ntok], ssps[:, :ntok], AF.Sqrt, bias=k.epsc[:], scale=1.0 / D), reads=[ssps, k.epsc], writes=[rstd])
    P.op("dve", lambda e: e.reciprocal(rstd[:, :ntok], rstd[:, :ntok]), reads=[rstd], writes=[rstd])
    for kc in range(KC):
        t = tmp[kc % 2]
        P.op("dve", lambda e, kc=kc, t=t: e.scalar_tensor_tensor(t[:, :ntok], xT[:, kc, :ntok], svec[:, kc:kc + 1], rstd[:, :ntok], ALU.mult, ALU.mult),
             reads=[xT, svec, rstd], writes=[t])
        if shvec is not None:
            P.op("pool", lambda e, kc=kc, t=t: e.tensor_scalar(hT[:, kc, :ntok], t[:, :ntok], shvec[:, kc:kc + 1], None, ALU.add),
                 reads=[t, shvec], writes=[hT])
        else:
            P.op("pool", lambda e, kc=kc, t=t: e.tensor_copy(hT[:, kc, :ntok], t[:, :ntok]), reads=[t], writes=[hT])


def load_xT(P, k, src_rows, tok0, ntok, xT, pools):
    xr, tps = pools['xr'], pools['tps']
    nsub = ntok // 128
    for j in range(nsub):
        P.dma("sp", xr[j][:], src_rows[tok0 + j * 128: tok0 + (j + 1) * 128, :], writes=[xr[j]])
    for kc in range(KC):
        pt = tps[kc % 2]
        for j in range(nsub):
            P.op("pe", lambda e, kc=kc, j=j, pt=pt: e.transpose(pt[:, j * 128:(j + 1) * 128], xr[j][:, kc * 128:(kc + 1) * 128], k.ident[:]),
                 reads=[xr[j], k.ident], writes=[pt])
        if kc % 2 == 0:
            P.op("act", lambda e, kc=kc, pt=pt: e.activation(xT[:, kc, :ntok], pt[:, :ntok], AF.Copy), reads=[pt], writes=[xT])
        else:
            P.op("dve", lambda e, kc=kc, pt=pt: e.tensor_copy(xT[:, kc, :ntok], pt[:, :ntok]), reads=[pt], writes=[xT])


def phase1(P, k, I, pfm, abtm=None):
    W = P.sb("w_in", [128, KC, NCT * 128], BF16)
    wv = I['w_in'].rearrange("(c p) n -> p c n", p=128)
    for kc in range(KC):
        for hh in range(2):
            c0 = hh * 1216
            P.dma("pool", W[:, kc, c0:c0 + 1216], wv[:, kc, c0:c0 + 1216], writes=[W], group=True)
    pools = dict(
        xr=[P.sb("xr", [128, D]) for _ in range(4)],
        tps=[P.ps("tps", [128, 512]) for _ in range(2)],
        sq=[P.sb("sq", [128, 512], BF16) for _ in range(2)],
        ssps=P.ps("ssps", [128, 512]),
        rstd=P.sb("rstd", [128, 512]),
        tmp=[P.sb("tmp", [128, 512]) for _ in range(2)],
    )
    k.pools = pools
    P.share(pools['xr'], "xr"); P.share(stg, "stg") if False else None
    xT = P.sb("xT", [128, KC, 512])
    hT = P.sb("hT", [128, KC, 512], BF16)
    ops = [P.ps("ops", [128, 512]) for _ in range(3)]
    stg = [P.sb("stg", [128, 512]) for _ in range(4)]
    blocks = [(0, 256, True)] + [(256 + 512 * i, 512, False) for i in range(32)]
    n = 0
    for (tok0, ntok, isctx) in blocks:
        load_xT(P, k, I['xcat'], tok0, ntok, xT, pools)
        norm_fm(P, k, xT, ntok, k.cs1 if isctx else k.s1, k.csh1 if isctx else k.sh1, hT, pools)
        for ct in range(NCT):
            pt = ops[n % 3]; sg = stg[n % 4]; n += 1
            for kc in range(KC):
                P.op("pe", lambda e, ct=ct, kc=kc, pt=pt: e.matmul(pt[:, :ntok], W[:, kc, ct * 128:(ct + 1) * 128], hT[:, kc, :ntok],
                                                                  start=(kc == 0), stop=(kc == KC - 1)),
                     reads=[W, hT], writes=[pt])
            if n % 2 == 0:
                P.op("act", lambda e, pt=pt, sg=sg: e.activation(sg[:, :ntok], pt[:, :ntok], AF.Copy), reads=[pt], writes=[sg])
            else:
                P.op("dve", lambda e, pt=pt, sg=sg: e.tensor_copy(sg[:, :ntok], pt[:, :ntok]), reads=[pt], writes=[sg])
            P.dma("sp", pfm[ct * 128:(ct + 1) * 128, tok0:tok0 + ntok], sg[:, :ntok], reads=[sg], writes=[pfm])
        if abtm is not None:
            for j in range(ntok // 128):
                pt = ops[n % 3]; sg = stg[n % 4]; n += 1
                for kc in range(KC):
                    P.op("pe", lambda e, kc=kc, pt=pt, j=j: e.matmul(pt[:, 0:8], hT[:, kc, j * 128:(j + 1) * 128], W[:, kc, 18 * 128:18 * 128 + 8],
                                                                  start=(kc == 0), stop=(kc == KC - 1)), reads=[W, hT], writes=[pt])
                P.op("act", lambda e, pt=pt, sg=sg: e.activation(sg[:, 0:8], pt[:, 0:8], AF.Copy), reads=[pt], writes=[sg])
                P.dma("sp", abtm.h[tok0 + j * 128: tok0 + (j + 1) * 128, :], sg[:, 0:8], reads=[sg], writes=[abtm])


def rwkv_inputs(inp, g, m):
    f = np.float32
    cs = slice(256 * g, 256 * g + 256)
    col2 = lambda v: np.ascontiguousarray(v.reshape(2, 128).T)
    mu = inp['rwkv_mu'][0]
    tiles = [mu[0:1024][cs][:128], mu[0:1024][cs][128:], mu[1024:2048][cs][:128], mu[1024:2048][cs][128:],
             mu[2048:3072][cs][:128], mu[2048:3072][cs][128:],
             np.concatenate([mu[3072:3168], np.zeros(32, f)]), mu[3168:3296], mu[3296:3424], mu[3424:3552]]
    m['muT'] = np.ascontiguousarray(np.stack(tiles, axis=1))
    qm = np.zeros((128, 4), f)
    for q in range(4):
        qm[q::4, q] = 1.0
    m['qmask'] = qm
    m['k_k'] = col2(inp['rwkv_k_k'][0][cs]); m['k_a'] = col2(inp['rwkv_k_a'][0][cs])
    m['r_k'] = col2(inp['rwkv_r_k'][0].reshape(-1)[cs])
    m['w0'] = np.ascontiguousarray(np.stack([col2(inp['rwkv_w0'][0][d][cs]) for d in range(2)], axis=1))
    m['a0'] = np.ascontiguousarray(np.stack([col2(inp['rwkv_a0'][0][d][cs]) for d in range(2)], axis=1))
    wu = np.zeros((2, 128, 256), f); wu[:, :96, :] = inp['rwkv_w_up'][0][:, :, cs]
    m['w_up'] = np.ascontiguousarray(wu.transpose(1, 0, 2))
    m['a_up'] = np.ascontiguousarray(inp['rwkv_a_up'][0][:, :, cs].transpose(1, 0, 2))
    m['g_up'] = np.ascontiguousarray(inp['rwkv_g_up'][0][:, cs].reshape(2, 128, 256).transpose(1, 0, 2))
    hm = np.zeros((128, 128), f); hm[:64, :64] = 1; hm[64:, 64:] = 1
    m['headones'] = hm
    hi = np.zeros((128, 2), f); hi[:64, 0] = 1; hi[64:, 1] = 1
    m['headind'] = hi
    rm = np.ones((128, 8, 64), f); rm[:, :, 0] = 0
    m['resetmask'] = rm
    m['ln_w'] = np.ascontiguousarray(np.broadcast_to(inp['rwkv_ln_w'][0][cs], (128, 256)))
    m['ln_b'] = np.ascontiguousarray(np.broadcast_to(inp['rwkv_ln_b'][0][cs], (128, 256)))


def rwkv_prep(P, k, I, pfm, S):
    ld = lambda name, shape, dt=F32, q="sp": _ld(P, I, name, shape, dt, q)
    muT = ld('muT', [128, 10]); qmask = ld('qmask', [128, 4])
    k_k = ld('k_k', [128, 2]); k_a = ld('k_a', [128, 2]); r_k = ld('r_k', [128, 2])
    w0 = ld('w0', [128, 2, 2]); a0 = ld('a0', [128, 2, 2])
    w_up = ld('w_up', [128, 2, 256], BF16, "pool"); a_up = ld('a_up', [128, 2, 256], BF16, "pool")
    g_up = ld('g_up', [128, 2, 256], BF16, "pool")
    hones = ld('headones', [128, 128], BF16, "pool"); hind = ld('headind', [128, 2], BF16, "pool")
    rmask = ld('resetmask', [128, 8, 64])
    om = P.sb("om", [128, 10])
    P.op("dve", lambda e: e.tensor_scalar(om[:], muT[:], -1.0, 1.0, ALU.mult, ALU.add), reads=[muT], writes=[om])
    muq = P.sb("muq", [128, 10, 4])
    for q in range(4):
        P.op("dve", lambda e, q=q: e.tensor_scalar(muq[:, :, q], muT[:], qmask[:, q:q + 1], None, ALU.mult), reads=[muT, qmask], writes=[muq])
    oka = P.sb("oka", [128, 2])
    P.op("dve", lambda e: e.tensor_scalar(oka[:], k_a[:], -1.0, 1.0, ALU.mult, ALU.add), reads=[k_a], writes=[oka])
    eps12 = P.sb("eps12", [128, 1]); P.op("pool", lambda e: e.memset(eps12[:], 1e-12), writes=[eps12])

    raw = [P.sb("raw", [128, 10, 64]) for _ in range(2)]
    z = [P.sb("z", [128, 8, 64]) for _ in range(10)]
    pp = [P.ps("pp", [128, 512]) for _ in range(4)]
    ptm = [P.ps("ptm", [128, 256], BF16) for _ in range(2)]
    cnt = [0]

    def PS():
        cnt[0] += 1
        return pp[cnt[0] % 4]
    tmpf = [P.sb("tmpf", [128, 512]) for _ in range(6)]
    tb = [P.sb("tb", [128, 512], BF16) for _ in range(4)]
    kk = P.sb("kk", [128, 2, 512]); kd = P.sb("kd", [128, 2, 512]); bd = P.sb("bd", [128, 2, 512])
    aT = P.sb("aT", [128, 2, 512]); lw = P.sb("lw", [128, 2, 512]); ksum = P.sb("ksum", [128, 2, 512])
    pf = P.sb("pf", [128, 8, 64]); sx = P.sb("sx", [128, 8, 64])
    twb = P.sb("twb", [128, 512], BF16); xab = P.sb("xab", [128, 512], BF16); sgb = P.sb("sgb", [128, 2, 512], BF16)
    ofm = [P.sb("ofm", [128, 512], BF16) for _ in range(4)]
    otm = [P.sb("otm", [128, 256], BF16) for _ in range(4)]
    gam = P.sb("gam", [128, 2, 2, 8])
    gtm = P.sb("gtm", [128, 256]); btm = P.sb("btm", [128, 4])
    P.share(ofm, "ofm"); P.share([gam, gtm, btm], "gmisc"); P.share(raw, "raw")
    P.share([muT, qmask, k_k, k_a, r_k, w0, a0, rmask], "rc1"); P.share([w_up, a_up, g_up, hones, hind], "rc2")
    ocnt = [0]

    def flat(t):
        return t[:].rearrange("p a b -> p (a b)") if len(t.h.shape) == 3 else t[:]

    import os
    blocks = [(0, 256, True)] + [(256 + 512 * i, 512, False) for i in range(32)]
    blocks = blocks[:int(os.environ.get("RW_BLOCKS", "99"))]
    RW_STOP = int(os.environ.get("RW_STOP", "9"))
    for bidx, (tok0, ntok, isctx) in enumerate(blocks):
        if bidx % 8 == 7:
            P.barrier()
        nch = ntok // 64
        for ti in range(10):
            R = raw[ti % 2]
            prow = ti * 128
            if isctx:
                P.dma("sp", R[:, 1:5, :], pfm.h[prow:prow + 128, 0:256].rearrange("p (a b) -> p a b", b=64), reads=[pfm], writes=[R])
            else:
                lo = tok0 - 64; hi = tok0 + 576
                a0_ = 0; a1_ = 10
                if lo < 256:
                    P.op("pool", lambda e, R=R: e.memset(R[:, 0, :], 0.0), writes=[R]); a0_ = 1; lo = 256
                if hi > TT:
                    P.op("pool", lambda e, R=R: e.memset(R[:, 9, :], 0.0), writes=[R]); a1_ = 9; hi = TT
                P.dma("sp", R[:, a0_:a1_, :], pfm.h[prow:prow + 128, lo:hi].rearrange("p (a b) -> p a b", b=64), reads=[pfm], writes=[R])
            Z = z[ti]
            c0 = 1
            P.op("act", lambda e, R=R, Z=Z, ti=ti: e.activation(Z[:, :nch, :], R[:, c0:c0 + nch, :], AF.Identity, scale=om[:, ti:ti + 1]),
                 reads=[R, om], writes=[Z])
            def stt(o_ap, i_ap, q, ti=ti, R=R, Z=Z):
                P.op("dve", lambda e: e.scalar_tensor_tensor(o_ap, i_ap, muq[:, ti, q:q + 1], o_ap, ALU.mult, ALU.add), reads=[R, muq, Z], writes=[Z])
            if isctx:
                Zf = Z[:, 0:4, :].rearrange("p a b -> p (a b)"); Rf = R[:, 1:5, :].rearrange("p a b -> p (a b)")
                for q in (0, 2):
                    stt(Zf[:, 1:256], Rf[:, 0:255], q)
                for q in (1, 3):
                    stt(Zf[:, 0:255], Rf[:, 1:256], q)
            else:
                stt(Z[:, :, 1:64], R[:, 1:9, 0:63], 0)
                stt(Z[:, :, 0:63], R[:, 1:9, 1:64], 1)
                stt(Z[:, :, :], R[:, 0:8, :], 2)
                stt(Z[:, :, :], R[:, 2:10, :], 3)
        zf = [zz[:].rearrange("p a b -> p (a b)") for zz in z]
        N = ntok
        if RW_STOP < 1:
            continue
        P.op("act", lambda e: e.activation(twb[:, :N], zf[6][:, :N], AF.Tanh), reads=[z[6]], writes=[twb])
        P.op("pool", lambda e: e.tensor_copy(xab[:, :N], zf[7][:, :N]), reads=[z[7]], writes=[xab])
        for c in range(2):
            P.op("act", lambda e, c=c: e.activation(sgb[:, c, :N], zf[8 + c][:, :N], AF.Sigmoid), reads=[z[8 + c]], writes=[sgb])
        for c in range(2):
            t0_ = tmpf[0]
            P.op("dve", lambda e, c=c: e.tensor_scalar(kk[:, c, :N], zf[2 + c][:, :N], k_k[:, c:c + 1], None, ALU.mult), reads=[z[2 + c], k_k], writes=[kk])
            P.op("act", lambda e, c=c: e.activation(tb[0][:, :N], kk[:, c, :N], AF.Square), reads=[kk], writes=[tb[0]])
            ps_ = PS()
            P.op("pe", lambda e, ps_=ps_: e.matmul(ps_[:, :N], hones[:], tb[0][:, :N], start=True, stop=True), reads=[hones, tb[0]], writes=[ps_])
            P.op("act", lambda e, ps_=ps_: e.activation(t0_[:, :N], ps_[:, :N], AF.Sqrt, bias=eps12[:], scale=1.0), reads=[ps_, eps12], writes=[t0_])
            P.op("dve", lambda e: e.reciprocal(t0_[:, :N], t0_[:, :N]), reads=[t0_], writes=[t0_])
            P.op("dve", lambda e, c=c: e.tensor_tensor(kk[:, c, :N], kk[:, c, :N], t0_[:, :N], ALU.mult), reads=[kk, t0_], writes=[kk])
        if RW_STOP < 2:
            continue
        for d in range(2):
            for c in range(2):
                ps_ = PS()
                P.op("pe", lambda e, ps_=ps_, c=c, d=d: e.matmul(ps_[:, :N], w_up[:, d, c * 128:(c + 1) * 128], twb[:, :N], start=True, stop=True),
                     reads=[w_up, twb], writes=[ps_])
                P.op("act", lambda e, ps_=ps_, c=c, d=d: e.activation(lw[:, c, :N], ps_[:, :N], AF.Sigmoid, bias=w0[:, d, c:c + 1], scale=1.0),
                     reads=[ps_, w0], writes=[lw])
                P.op("pool", lambda e, c=c: e.tensor_scalar(lw[:, c, :N], lw[:, c, :N], -0.6065306597126334, None, ALU.mult), reads=[lw], writes=[lw])
                ps2 = PS()
                P.op("pe", lambda e, ps2=ps2, c=c, d=d: e.matmul(ps2[:, :N], a_up[:, d, c * 128:(c + 1) * 128], xab[:, :N], start=True, stop=True),
                     reads=[a_up, xab], writes=[ps2])
                P.op("act", lambda e, ps2=ps2, c=c, d=d: e.activation(aT[:, c, :N], ps2[:, :N], AF.Sigmoid, bias=a0[:, d, c:c + 1], scale=1.0),
                     reads=[ps2, a0], writes=[aT])
                P.op("dve", lambda e, c=c: e.tensor_scalar(kd[:, c, :N], aT[:, c, :N], k_a[:, c:c + 1], oka[:, c:c + 1], ALU.mult, ALU.add),
                     reads=[aT, k_a, oka], writes=[kd])
                P.op("dve", lambda e, c=c: e.tensor_tensor(kd[:, c, :N], kd[:, c, :N], zf[2 + c][:, :N], ALU.mult), reads=[kd, z[2 + c]], writes=[kd])
                P.op("pool", lambda e, c=c: e.tensor_tensor(bd[:, c, :N], aT[:, c, :N], kk[:, c, :N], ALU.mult), reads=[aT, kk], writes=[bd])
                if d == 0:
                    P.op("pool", lambda e, c=c: e.tensor_copy(ksum[:, c, :N], kd[:, c, :N]), reads=[kd], writes=[ksum])
                else:
                    P.op("pool", lambda e, c=c: e.tensor_tensor(ksum[:, c, :N], ksum[:, c, :N], kd[:, c, :N], ALU.add), reads=[kd, ksum], writes=[ksum])
                lwc = lw[:, c, :N]
                P.op("dve", lambda e, lwc=lwc: e.tensor_tensor_scan(flat(pf)[:, :N], flat(rmask)[:, :N], lwc, 0.0, ALU.mult, ALU.add),
                     reads=[rmask, lw], writes=[pf])
                for ch in range(nch):
                    P.op("pool", lambda e, ch=ch: e.tensor_scalar(sx[:, ch, :], pf[:, ch, :], -1.0, pf[:, ch, 63:64], ALU.mult, ALU.add),
                         reads=[pf], writes=[sx])
                if d == 0:
                    cum, rem = pf, sx
                    P.op("act", lambda e, c=c, d=d: e.activation(gam[:, c, d, :nch], pf[:, :nch, 63], AF.Exp), reads=[pf], writes=[gam])
                else:
                    P.op("dve", lambda e, lwc=lwc: e.tensor_tensor(flat(sx)[:, :N], flat(sx)[:, :N], lwc, ALU.add), reads=[sx, lw], writes=[sx])
                    P.op("act", lambda e, c=c, d=d: e.activation(gam[:, c, d, :nch], pf[:, :nch, 63], AF.Exp), reads=[pf], writes=[gam])
                    P.op("dve", lambda e, lwc=lwc: e.tensor_tensor(flat(pf)[:, :N], flat(pf)[:, :N], lwc, ALU.subtract), reads=[pf, lw], writes=[pf])
                    cum, rem = sx, pf
                cumf, remf = flat(cum)[:, :N], flat(rem)[:, :N]
                e_p, e_n, e_r, e_k = tmpf[1], tmpf[2], tmpf[3], tmpf[4]
                P.op("act", lambda e: e.activation(e_p[:, :N], cumf, AF.Exp), reads=[cum], writes=[e_p])
                P.op("act", lambda e: e.activation(e_n[:, :N], cumf, AF.Exp, scale=-1.0), reads=[cum], writes=[e_n])
                P.op("act", lambda e: e.activation(e_r[:, :N], remf, AF.Exp), reads=[rem], writes=[e_r])
                P.op("dve", lambda e, lwc=lwc: e.tensor_tensor(e_k[:, :N], cumf, lwc, ALU.subtract), reads=[cum, lw], writes=[e_k])
                P.op("act", lambda e: e.activation(e_k[:, :N], e_k[:, :N], AF.Exp), reads=[e_k], writes=[e_k])

                def out_fm(name, a_ap, b_ap, rd):
                    o = ofm[ocnt[0] % 4]; ocnt[0] += 1
                    P.op("dve", lambda e: e.tensor_tensor(o[:, :N], a_ap, b_ap, ALU.mult), reads=rd, writes=[o])
                    P.dma("sp", S[name].h[d, c * 128:(c + 1) * 128, tok0:tok0 + N], o[:, :N], reads=[o], writes=[S[name]])
                    return o

                def out_tm(name, src_bf, neg=False):
                    for j in range(N // 128):
                        pt = ptm[ocnt[0] % 2]; o = otm[ocnt[0] % 4]; ocnt[0] += 1
                        P.op("pe", lambda e, pt=pt, j=j: e.transpose(pt[:, 0:128], src_bf[:, j * 128:(j + 1) * 128], k.identb[:]),
                             reads=[src_bf, k.identb], writes=[pt])
                        P.op("act", lambda e, pt=pt, o=o: e.activation(o[:, 0:128], pt[:, 0:128], AF.Copy, scale=(-1.0 if neg else 1.0)), reads=[pt], writes=[o])
                        P.dma("act", S[name].h[d, tok0 + j * 128: tok0 + (j + 1) * 128, c * 128:(c + 1) * 128], o[:, 0:128], reads=[o], writes=[S[name]])
                out_fm('rt', zf[0 + c][:, :N], e_p[:, :N], [z[c], e_p])
                okap = out_fm('kap', kk[:, c, :N], e_k[:, :N], [kk, e_k])
                out_tm('kap_tm', okap)
                out_fm('kt', kd[:, c, :N], e_n[:, :N], [kd, e_n])
                out_fm('bt', bd[:, c, :N], e_n[:, :N], [bd, e_n])
                o1 = ofm[ocnt[0] % 4]; ocnt[0] += 1
                P.op("dve", lambda e, o1=o1, c=c: e.tensor_tensor(o1[:, :N], kd[:, c, :N], e_r[:, :N], ALU.mult), reads=[kd, e_r], writes=[o1])
                out_tm('kend_tm', o1)
                o2 = ofm[ocnt[0] % 4]; ocnt[0] += 1
                P.op("dve", lambda e, o2=o2, c=c: e.tensor_tensor(o2[:, :N], bd[:, c, :N], e_r[:, :N], ALU.mult), reads=[bd, e_r], writes=[o2])
                out_tm('bend_tm', o2, neg=True)
            for c in range(2):
                P.dma("sp", S['gam'].h[d, c * 128:(c + 1) * 128, tok0 // 64: tok0 // 64 + nch], gam[:, c, d, :nch], reads=[gam], writes=[S['gam']])
        if RW_STOP < 3:
            continue
        if not isctx:
            for j in range(N // 128):
                tk = tok0 + j * 128
                for c in range(2):
                    vb = tb[1]
                    P.op("pool", lambda e, c=c, j=j: e.tensor_copy(vb[:, 0:128], zf[4 + c][:, j * 128:(j + 1) * 128]), reads=[z[4 + c]], writes=[vb])
                    pt = ptm[ocnt[0] % 2]; o = otm[ocnt[0] % 4]; ocnt[0] += 1
                    P.op("pe", lambda e, pt=pt: e.transpose(pt[:, 0:128], vb[:, 0:128], k.identb[:]), reads=[vb, k.identb], writes=[pt])
                    P.op("act", lambda e, pt=pt, o=o: e.activation(o[:, 0:128], pt[:, 0:128], AF.Copy), reads=[pt], writes=[o])
                    P.dma("sp", S['v_tm'].h[tk:tk + 128, c * 128:(c + 1) * 128], o[:, 0:128], reads=[o], writes=[S['v_tm']])
                ps_ = PS()
                for c in range(2):
                    P.op("pe", lambda e, ps_=ps_, c=c, j=j: e.matmul(ps_[:, 0:256], sgb[:, c, j * 128:(j + 1) * 128], g_up[:, c, :], start=(c == 0), stop=(c == 1)),
                         reads=[sgb, g_up], writes=[ps_])
                P.op("act", lambda e, ps_=ps_: e.activation(gtm[:, :], ps_[:, 0:256], AF.Copy), reads=[ps_], writes=[gtm])
                P.dma("sp", S['gate_tm'].h[tk:tk + 128, :], gtm[:, :], reads=[gtm], writes=[S['gate_tm']])
            for c in range(2):
                rk = tmpf[5]
                P.op("dve", lambda e, c=c: e.scalar_tensor_tensor(rk[:, :N], zf[c][:, :N], r_k[:, c:c + 1], ksum[:, c, :N], ALU.mult, ALU.mult),
                     reads=[z[c], r_k, ksum], writes=[rk])
                rkb = tb[2]
                P.op("pool", lambda e: e.tensor_copy(rkb[:, :N], rk[:, :N]), reads=[rk], writes=[rkb])
                for j in range(N // 128):
                    tk = tok0 + j * 128
                    ps_ = PS()
                    P.op("pe", lambda e, ps_=ps_, j=j: e.matmul(ps_[:, 0:2], rkb[:, j * 128:(j + 1) * 128], hind[:, :], start=True, stop=True),
                         reads=[rkb, hind], writes=[ps_])
                    P.op("act", lambda e, ps_=ps_: e.activation(btm[:, 0:2], ps_[:, 0:2], AF.Copy), reads=[ps_], writes=[btm])
                    P.dma("sp", S['bonus_tm'].h[tk:tk + 128, 2 * c:2 * c + 2], btm[:, 0:2], reads=[btm], writes=[S['bonus_tm']])
        else:
            for j in range(N // 128):
                tk = tok0 + j * 128
                for c in range(2):
                    vb = tb[1]
                    P.op("pool", lambda e, c=c, j=j: e.tensor_copy(vb[:, 0:128], zf[4 + c][:, j * 128:(j + 1) * 128]), reads=[z[4 + c]], writes=[vb])
                    pt = ptm[ocnt[0] % 2]; o = otm[ocnt[0] % 4]; ocnt[0] += 1
                    P.op("pe", lambda e, pt=pt: e.transpose(pt[:, 0:128], vb[:, 0:128], k.identb[:]), reads=[vb, k.identb], writes=[pt])
                    P.op("act", lambda e, pt=pt, o=o: e.activation(o[:, 0:128], pt[:, 0:128], AF.Copy), reads=[pt], writes=[o])
                    P.dma("sp", S['v_tm'].h[tk:tk + 128, c * 128:(c + 1) * 128], o[:, 0:128], reads=[o], writes=[S['v_tm']])


def _ld(P, I, name, shape, dt=F32, q="sp"):
    t = P.sb(name, shape, dt)
    P.dma(q, t[:], I[name], writes=[t])
    return t


def rwkv_scratch(P):
    S = {}
    for n in ('rt', 'kap', 'kt', 'bt'):
        S[n] = P.dram("rw_" + n, [2, 256, TT], BF16)
    for n in ('kap_tm', 'kend_tm', 'bend_tm'):
        S[n] = P.dram("rw_" + n, [2, TT, 256], BF16)
    S['v_tm'] = P.dram("rw_v_tm", [TT, 256], BF16)
    S['gate_tm'] = P.dram("rw_gate_tm", [TT, 256], F32)
    S['bonus_tm'] = P.dram("rw_bonus_tm", [TT, 4], F32)
    S['gam'] = P.dram("rw_gam", [2, 256, TT // 64], F32)
    return S


def gdn_inputs(inp, g, m):
    f = np.float32
    cs = slice(256 * g, 256 * g + 256)
    cw = inp['gdn_conv_w'][0]
    tiles = []
    for sec in range(3):
        for h in range(2):
            tiles.append(cw[:, sec * 1024 + 256 * g + 128 * h: sec * 1024 + 256 * g + 128 * (h + 1)].T)
    m['convw'] = np.ascontiguousarray(np.stack(tiles, axis=1))
    hs = [2 * g, 2 * g + 1]
    al = np.array([inp['gdn_a_log'][0][d][h] for d in range(2) for h in hs], f)
    db = np.array([inp['gdn_dt_bias'][0][d][h] for d in range(2) for h in hs], f)
    m['alog_row'] = np.ascontiguousarray(np.broadcast_to(al, (128, 4)))
    m['dtb_row'] = np.ascontiguousarray(np.broadcast_to(db, (128, 4)))
    m['gnw_row'] = np.ascontiguousarray(np.broadcast_to(np.tile(inp['gdn_norm_w'][0], 2), (128, 256)))
    idx = np.arange(128); same = (idx[:, None] // 64) == (idx[None, :] // 64)
    for d in range(2):
        before = (idx[:, None] < idx[None, :]) if d == 0 else (idx[:, None] > idx[None, :])
        strict = (before & same).astype(f); incl = ((before | np.eye(128, dtype=bool)) & same).astype(f)
        m[f'mk_strict{d}'] = strict; m[f'mk_incl{d}'] = incl
        m[f'mk_nstrict{d}'] = -strict; m[f'mk_nincl{d}'] = -incl
        m[f'mk_nstrictT{d}'] = np.ascontiguousarray(-strict.T)
        m[f'mk_strictT{d}'] = np.ascontiguousarray(strict.T)
        m[f'tri_incl{d}'] = incl
        after = np.ascontiguousarray(strict.T)
        m[f'tri_after{d}'] = after
    m['onesf'] = np.ones((128, 128), f)


class Chain:
    pass


def sweep(P, k, I, pfm, S, Y, d, abtm=None):
    ldc = lambda name, dt=BF16: _ld(P, I, name, [128, 128], dt, "pool" if dt == BF16 else "sp")
    mk_strict = ldc(f'mk_strict{d}', F32); mk_incl = ldc(f'mk_incl{d}', F32)
    mk_nstrict = ldc(f'mk_nstrict{d}', F32); mk_nincl = ldc(f'mk_nincl{d}', F32)
    mk_nstrictT = ldc(f'mk_nstrictT{d}', F32); mk_strictT = ldc(f'mk_strictT{d}', F32)
    tri_incl = ldc(f'tri_incl{d}', F32); tri_after = ldc(f'tri_after{d}', F32)
    onesf = ldc('onesf', F32)
    convw = _ld(P, I, 'convw', [128, 6, 3]); alog = _ld(P, I, 'alog_row', [128, 4]); dtb = _ld(P, I, 'dtb_row', [128, 4])
    P.share([mk_strict, mk_incl, mk_nstrict, mk_nincl, mk_nstrictT, mk_strictT, tri_incl, tri_after, onesf, convw, alog, dtb], "swc")
    nega = P.sb("nega", [128, 4])
    P.op("act", lambda e: e.activation(nega[:], alog[:], AF.Exp), reads=[alog], writes=[nega])
    P.op("dve", lambda e: e.tensor_scalar(nega[:], nega[:], -1.0, None, ALU.mult), reads=[nega], writes=[nega])
    eps12 = P.sb("eps12s", [128, 1]); P.op("pool", lambda e: e.memset(eps12[:], 1e-12), writes=[eps12])

    NH = 6
    dks = [64] * 4 + [128] * 2
    H32 = [P.sb("H32", [dks[h], dks[h]]) for h in range(NH)]
    Hbf = [P.sb("Hbf", [dks[h], dks[h]], BF16) for h in range(NH)]
    for h in range(NH):
        P.op("pool", lambda e, h=h: e.memset(H32[h][:], 0.0), writes=[H32[h]])
        P.op("pool", lambda e, h=h: e.memset(Hbf[h][:], 0.0), writes=[Hbf[h]])
    def mk(name, shape, dt=BF16, n=2):
        return [[P.sb(name, shape, dt) for _ in range(n)] for _ in range(NH)]
    QA = mk("QA", [128, 256]); KB = mk("KB", [128, 256]); KAPtm = mk("KAPtm", [128, 128]); KEND = mk("KEND", [128, 128]); BEND = mk("BEND", [128, 128])
    Vtm = mk("Vtm", [128, 128]); GAM = mk("GAM", [128, 2], F32)
    QG = mk("QG", [128, 128])
    M_Akk = mk("M_Akk", [128, 128]); M_Ark = mk("M_Ark", [128, 128]); M_Arb = mk("M_Arb", [128, 128])
    Xa = mk("Xa", [128, 256], BF16, 1); Xb = mk("Xb", [128, 256], BF16, 1)
    Q32 = mk("Q32", [128, 128], F32, 1); Qb = mk("Qb", [128, 128], BF16, 1)
    AVb = mk("AVb", [128, 128], BF16, 1); U0 = mk("U0", [128, 128], F32); WKT = mk("WKT", [128, 128])
    Usb = mk("Usb", [128, 128], BF16, 1); Yt = mk("Yt", [128, 128], F32)
    for par_ in range(2):
        P.share([t[h][par_] for t in (QA, KB, KAPtm, KEND, BEND, Vtm, GAM) for h in range(4)], "swl")
        P.share([Yt[h][par_] for h in range(NH)], "swy")
    pA = [P.ps("pA", [128, 512]) for _ in range(3)]
    pB = [T("pB", P.ps("pB", [128, 512]).h[:, 0:256], "ps") for _ in range(2)]
    pC = [T("pC", P.ps("pC", [128, 512]).h[:, 0:128], "ps") for _ in range(2)]
    pT = T("pT", P.ps("pT", [128, 1024], BF16).h[:, 0:256], "ps")
    cn = {'a': 0, 'b': 0, 'c': 0}

    def PA():
        cn['a'] += 1; return pA[cn['a'] % 3]

    def PB():
        cn['b'] += 1; return pB[cn['b'] % 2]

    def PC():
        cn['c'] += 1; return pC[cn['c'] % 2]
    graw = [P.sb("graw", [128, 130]) for _ in range(2)]
    P.share(graw, "graw")
    gcv = P.sb("gcv", [128, 128]); gsl = P.sb("gsl", [128, 128]); gsq = P.sb("gsq", [128, 128], BF16)
    gq = P.sb("gq", [128, 128]); gk = P.sb("gk", [128, 128]); gkb = P.sb("gkb", [128, 128], BF16); gvb = P.sb("gvb", [128, 128], BF16)
    gktm = P.sb("gktm", [128, 128]); gvtm = P.sb("gvtm", [128, 128])
    abr = P.sb("abr", [8, 128]); abT = P.sb("abT", [128, 8]); gt = P.sb("gt", [128, 4]); bt = P.sb("bt", [128, 4])
    gctm = P.sb("gctm", [128, 4]); remtm = P.sb("remtm", [128, 4]); ecol = P.sb("ecol", [128, 4]); ercol = P.sb("ercol", [128, 4])
    colb = P.sb("colb", [128, 128]); GCb = P.sb("GCb", [128, 128]); Bb = P.sb("Bb", [128, 128]); E = P.sb("E", [128, 128]); ET = P.sb("ET", [128, 128])
    M1 = P.sb("M1", [128, 128]); M1T = P.sb("M1T", [128, 128]); M2 = P.sb("M2", [128, 128]); eG = P.sb("eG", [128, 128])
    sc1 = P.sb("sc1", [128, 1]); rn = P.sb("rn", [128, 128])

    nblk = TT // 128
    order = ([0, 1] if d == 0 else [1, 0]) + (list(range(2, nblk)) if d == 0 else list(range(nblk - 1, 1, -1)))
    import os
    order = order[:int(os.environ.get("SW_BLOCKS", "999"))]
    SW_STOP = int(os.environ.get("SW_STOP", "9"))
    for bi, blk in enumerate(order):
        if bi % 16 == 15:
            P.barrier()
        par = bi % 2
        t0 = blk * 128
        isctx = blk < 2
        seq_lo = 0 if isctx else 256
        seq_hi = 256 if isctx else TT
        for h in range(4):
            c0 = h * 64
            qa, kb = QA[h][par], KB[h][par]
            P.dma("sp", qa[0:64, 0:128], S['kap'].h[d, c0:c0 + 64, t0:t0 + 128], reads=[S['kap']], writes=[qa], group=True)
            P.dma("sp", qa[0:64, 128:256], S['rt'].h[d, c0:c0 + 64, t0:t0 + 128], reads=[S['rt']], writes=[qa], group=True)
            P.dma("sp", kb[0:64, 0:128], S['kt'].h[d, c0:c0 + 64, t0:t0 + 128], reads=[S['kt']], writes=[kb], group=True)
            P.dma("sp", kb[0:64, 128:256], S['bt'].h[d, c0:c0 + 64, t0:t0 + 128], reads=[S['bt']], writes=[kb], group=True)
            P.dma("sp", KAPtm[h][par][:, 0:64], S['kap_tm'].h[d, t0:t0 + 128, c0:c0 + 64], reads=[S['kap_tm']], writes=[KAPtm[h][par]])
            P.dma("sp", KEND[h][par][:, 0:64], S['kend_tm'].h[d, t0:t0 + 128, c0:c0 + 64], reads=[S['kend_tm']], writes=[KEND[h][par]])
            P.dma("sp", BEND[h][par][:, 0:64], S['bend_tm'].h[d, t0:t0 + 128, c0:c0 + 64], reads=[S['bend_tm']], writes=[BEND[h][par]])
            P.dma("sp", Vtm[h][par][:, 0:64], S['v_tm'].h[t0:t0 + 128, c0:c0 + 64], reads=[S['v_tm']], writes=[Vtm[h][par]])
            P.dma("sp", GAM[h][par][0:64, :], S['gam'].h[d, c0:c0 + 64, blk * 2:blk * 2 + 2], reads=[S['gam']], writes=[GAM[h][par]])
        if SW_STOP < 1:
            continue
        P.dma("sp", abT[:, :], abtm.h[t0:t0 + 128, :], reads=[abtm], writes=[abT])
        P.op("dve", lambda e: e.tensor_tensor(gt[:], abT[:, 0:4], dtb[:], ALU.add), reads=[abT, dtb], writes=[gt])
        P.op("act", lambda e: e.activation(gt[:], gt[:], AF.Exp), reads=[gt], writes=[gt])
        P.op("act", lambda e: e.activation(gt[:], gt[:], AF.Ln, bias=k.onec[:], scale=1.0), reads=[gt, k.onec], writes=[gt])
        P.op("dve", lambda e: e.tensor_tensor(gt[:], gt[:], nega[:], ALU.mult), reads=[gt, nega], writes=[gt])
        P.op("act", lambda e: e.activation(bt[:], abT[:, 4:8], AF.Sigmoid), reads=[abT], writes=[bt])
        pc = PC()
        P.op("pe", lambda e, pc=pc: e.matmul(pc[:, 0:4], tri_incl[:], gt[:], start=True, stop=True), reads=[tri_incl, gt], writes=[pc])
        P.op("act", lambda e, pc=pc: e.activation(gctm[:], pc[:, 0:4], AF.Copy), reads=[pc], writes=[gctm])
        pc = PC()
        P.op("pe", lambda e, pc=pc: e.matmul(pc[:, 0:4], tri_after[:], gt[:], start=True, stop=True), reads=[tri_after, gt], writes=[pc])
        P.op("act", lambda e, pc=pc: e.activation(ercol[:], pc[:, 0:4], AF.Exp), reads=[pc], writes=[ercol])
        P.op("act", lambda e: e.activation(ecol[:], gctm[:], AF.Exp), reads=[gctm], writes=[ecol])
        for hh in range(2):
            h = 4 + hh
            j = d * 2 + hh
            qa, kb = QA[h][par], KB[h][par]
            res = {}
            for sec, nm in enumerate(('q', 'k', 'v')):
                ti = 10 + sec * 2 + hh
                R = graw[(sec + hh) % 2]
                lo, hi = t0 - 1, t0 + 129
                a0_, a1_ = 0, 130
                if lo < seq_lo:
                    P.op("pool", lambda e, R=R: e.memset(R[:, 0:1], 0.0), writes=[R]); a0_ = 1; lo = seq_lo
                if hi > seq_hi:
                    P.op("pool", lambda e, R=R: e.memset(R[:, 129:130], 0.0), writes=[R]); a1_ = 129; hi = seq_hi
                P.dma("sp", R[:, a0_:a1_], pfm.h[ti * 128:(ti + 1) * 128, lo:hi], reads=[pfm], writes=[R])
                wi = sec * 2 + hh
                P.op("dve", lambda e, R=R, wi=wi: e.tensor_scalar(gcv[:], R[:, 0:128], convw[:, wi, 0:1], None, ALU.mult), reads=[R, convw], writes=[gcv])
                P.op("dve", lambda e, R=R, wi=wi: e.scalar_tensor_tensor(gcv[:], R[:, 1:129], convw[:, wi, 1:2], gcv[:], ALU.mult, ALU.add), reads=[R, convw, gcv], writes=[gcv])
                P.op("dve", lambda e, R=R, wi=wi: e.scalar_tensor_tensor(gcv[:], R[:, 2:130], convw[:, wi, 2:3], gcv[:], ALU.mult, ALU.add), reads=[R, convw, gcv], writes=[gcv])
                if nm == 'v':
                    P.op("act", lambda e: e.activation(gvb[:], gcv[:], AF.Silu), reads=[gcv], writes=[gvb])
                else:
                    dst = gq if nm == 'q' else gk
                    P.op("act", lambda e, dst=dst: e.activation(dst[:], gcv[:], AF.Silu), reads=[gcv], writes=[dst])
                    P.op("act", lambda e, dst=dst: e.activation(gsq[:], dst[:], AF.Square), reads=[dst], writes=[gsq])
                    pc = PC()
                    P.op("pe", lambda e, pc=pc: e.matmul(pc[:, :], k.onesb[:], gsq[:], start=True, stop=True), reads=[k.onesb, gsq], writes=[pc])
                    P.op("act", lambda e, pc=pc: e.activation(rn[:], pc[:, :], AF.Sqrt, bias=eps12[:], scale=1.0), reads=[pc, eps12], writes=[rn])
                    P.op("dve", lambda e: e.reciprocal(rn[:], rn[:]), reads=[rn], writes=[rn])
                    if nm == 'q':
                        P.op("dve", lambda e: e.scalar_tensor_tensor(gq[:], gq[:], 128.0 ** -0.5, rn[:], ALU.mult, ALU.mult), reads=[gq, rn], writes=[gq])
                    else:
                        P.op("dve", lambda e: e.tensor_tensor(gk[:], gk[:], rn[:], ALU.mult), reads=[gk, rn], writes=[gk])
                        P.op("pool", lambda e: e.tensor_copy(gkb[:], gk[:]), reads=[gk], writes=[gkb])
            P.op("pe", lambda e: e.transpose(pT[:, 0:128], gkb[:], k.identb[:]), reads=[gkb, k.identb], writes=[pT])
            P.op("pe", lambda e: e.transpose(pT[:, 128:256], gvb[:], k.identb[:]), reads=[gvb, k.identb], writes=[pT])
            P.op("act", lambda e: e.activation(gktm[:], pT[:, 0:128], AF.Copy), reads=[pT], writes=[gktm])
            P.op("act", lambda e: e.activation(gvtm[:], pT[:, 128:256], AF.Copy), reads=[pT], writes=[gvtm])
            P.op("dve", lambda e, j=j: e.tensor_scalar(colb[:], onesf[:], gt[:, j:j + 1], None, ALU.mult), reads=[onesf, gt], writes=[colb])
            pc = PC()
            P.op("pe", lambda e, pc=pc: e.matmul(pc[:, :], colb[:], tri_incl[:], start=True, stop=True), reads=[colb, tri_incl], writes=[pc])
            P.op("act", lambda e, pc=pc: e.activation(GCb[:], pc[:, :], AF.Copy), reads=[pc], writes=[GCb])
            P.op("dve", lambda e, j=j: e.tensor_scalar(colb[:], onesf[:], bt[:, j:j + 1], None, ALU.mult), reads=[onesf, bt], writes=[colb])
            pc = PC()
            P.op("pe", lambda e, pc=pc: e.matmul(pc[:, :], colb[:], k.ident[:], start=True, stop=True), reads=[colb, k.ident], writes=[pc])
            P.op("act", lambda e, pc=pc: e.activation(Bb[:], pc[:, :], AF.Copy), reads=[pc], writes=[Bb])
            P.op("dve", lambda e, j=j: e.tensor_scalar(E[:], GCb[:], gctm[:, j:j + 1], 0.0, ALU.subtract, ALU.min), reads=[GCb, gctm], writes=[E])
            P.op("act", lambda e: e.activation(E[:], E[:], AF.Exp), reads=[E], writes=[E])
            P.op("dve", lambda e, j=j: e.tensor_scalar(ET[:], GCb[:], gctm[:, j:j + 1], -1.0, ALU.subtract, ALU.mult), reads=[GCb, gctm], writes=[ET])
            P.op("dve", lambda e: e.tensor_scalar(ET[:], ET[:], 0.0, None, ALU.min), reads=[ET], writes=[ET])
            P.op("act", lambda e: e.activation(ET[:], ET[:], AF.Exp), reads=[ET], writes=[ET])
            P.op("dve", lambda e: e.tensor_tensor(M1[:], E[:], mk_nstrict[:], ALU.mult), reads=[E, mk_nstrict], writes=[M1])
            P.op("dve", lambda e: e.tensor_tensor(M1[:], M1[:], Bb[:], ALU.mult), reads=[M1, Bb], writes=[M1])
            P.op("dve", lambda e, j=j: e.scalar_tensor_tensor(M1T[:], ET[:], bt[:, j:j + 1], mk_nstrictT[:], ALU.mult, ALU.mult), reads=[ET, bt, mk_nstrictT], writes=[M1T])
            P.op("pool", lambda e: e.tensor_tensor(M2[:], E[:], mk_incl[:], ALU.mult), reads=[E, mk_incl], writes=[M2])
            P.op("act", lambda e: e.activation(eG[:], GCb[:], AF.Exp), reads=[GCb], writes=[eG])
            P.op("pool", lambda e, qa=qa: e.tensor_copy(qa[:, 0:128], gk[:]), reads=[gk], writes=[qa])
            P.op("pool", lambda e, qa=qa: e.tensor_copy(qa[:, 128:256], gq[:]), reads=[gq], writes=[qa])
            P.op("dve", lambda e, h=h: e.tensor_tensor(QG[h][par][:], gq[:], eG[:], ALU.mult), reads=[gq, eG], writes=[QG[h][par]])
            P.op("pool", lambda e, kb=kb: e.tensor_copy(kb[:, 0:128], gk[:]), reads=[gk], writes=[kb])
            P.op("dve", lambda e, j=j, h=h: e.tensor_scalar(Vtm[h][par][:], gvtm[:], bt[:, j:j + 1], None, ALU.mult), reads=[gvtm, bt], writes=[Vtm[h][par]])
            P.op("dve", lambda e, j=j: e.scalar_tensor_tensor(sc1[:], bt[:, j:j + 1], -1.0, ecol[:, j:j + 1], ALU.mult, ALU.mult), reads=[bt, ecol], writes=[sc1])
            P.op("dve", lambda e, h=h: e.tensor_scalar(KAPtm[h][par][:], gktm[:], sc1[:, 0:1], None, ALU.mult), reads=[gktm, sc1], writes=[KAPtm[h][par]])
            P.op("dve", lambda e, j=j, h=h: e.tensor_scalar(BEND[h][par][:], gktm[:], ercol[:, j:j + 1], None, ALU.mult), reads=[gktm, ercol], writes=[BEND[h][par]])
            e0, e1 = (63, 127) if d == 0 else (0, 64)
            P.op("act", lambda e, h=h: e.activation(GAM[h][par][:, 0:1], GCb[:, e0:e0 + 1], AF.Exp), reads=[GCb], writes=[GAM[h][par]])
            P.op("act", lambda e, h=h: e.activation(GAM[h][par][:, 1:2], GCb[:, e1:e1 + 1], AF.Exp), reads=[GCb], writes=[GAM[h][par]])
            pa = PA()
            P.op("pe", lambda e, pa=pa, qa=qa, kb=kb: e.matmul(pa[:, 0:256], kb[:, 0:128], qa[:, 0:256], start=True, stop=True), reads=[kb, qa], writes=[pa])
            xa, xb = Xa[h][0], Xb[h][0]
            P.op("dve", lambda e, pa=pa, xa=xa: e.tensor_tensor(xa[:, 0:128], pa[:, 0:128], M1[:], ALU.mult), reads=[pa, M1], writes=[xa])
            P.op("dve", lambda e, pa=pa, xa=xa: e.tensor_tensor(xa[:, 128:256], pa[:, 0:128], M1T[:], ALU.mult), reads=[pa, M1T], writes=[xa])
            P.op("dve", lambda e, pa=pa, h=h: e.tensor_tensor(M_Arb[h][par][:], pa[:, 128:256], M2[:], ALU.mult), reads=[pa, M2], writes=[M_Arb[h][par]])
        if SW_STOP < 2:
            continue
        for h in range(4):
            qa, kb = QA[h][par], KB[h][par]
            xa = Xa[h][0]
            pa = PA()
            P.op("pe", lambda e, pa=pa, qa=qa, kb=kb: e.matmul(pa[:, 0:256], kb[0:64, 0:128], qa[0:64, 0:256], start=True, stop=True), reads=[kb, qa], writes=[pa])
            P.op("dve", lambda e, pa=pa, h=h: e.tensor_tensor(M_Akk[h][par][:], pa[:, 0:128], mk_strict[:], ALU.mult), reads=[pa, mk_strict], writes=[M_Akk[h][par]])
            P.op("dve", lambda e, pa=pa, h=h: e.tensor_tensor(M_Ark[h][par][:], pa[:, 128:256], mk_incl[:], ALU.mult), reads=[pa, mk_incl], writes=[M_Ark[h][par]])
            pa2 = PA()
            P.op("pe", lambda e, pa2=pa2, qa=qa, kb=kb: e.matmul(pa2[:, 0:256], kb[0:64, 128:256], qa[0:64, 0:256], start=True, stop=True), reads=[kb, qa], writes=[pa2])
            P.op("pe", lambda e, pa2=pa2, qa=qa, kb=kb: e.matmul(pa2[:, 256:384], qa[0:64, 0:128], kb[0:64, 128:256], start=True, stop=True), reads=[kb, qa], writes=[pa2])
            P.op("dve", lambda e, pa2=pa2, xa=xa: e.tensor_tensor(xa[:, 0:128], pa2[:, 0:128], mk_nstrict[:], ALU.mult), reads=[pa2, mk_nstrict], writes=[xa])
            P.op("dve", lambda e, pa2=pa2, h=h: e.tensor_tensor(M_Arb[h][par][:], pa2[:, 128:256], mk_nincl[:], ALU.mult), reads=[pa2, mk_nincl], writes=[M_Arb[h][par]])
            P.op("dve", lambda e, pa2=pa2, xa=xa: e.tensor_tensor(xa[:, 128:256], pa2[:, 256:384], mk_nstrictT[:], ALU.mult), reads=[pa2, mk_nstrictT], writes=[xa])
        if SW_STOP < 3:
            continue
        cur = [Xa[h][0] for h in range(NH)]; oth = [Xb[h][0] for h in range(NH)]
        for h in range(NH):
            x0_ = cur[h]
            P.op("dve", lambda e, h=h, x0_=x0_: e.tensor_tensor(Q32[h][0][:], x0_[:, 0:128], k.ident[:], ALU.add), reads=[x0_, k.ident], writes=[Q32[h][0]])
            P.op("pool", lambda e, h=h: e.tensor_copy(Qb[h][0][:], Q32[h][0][:]), reads=[Q32[h][0]], writes=[Qb[h][0]])
        for lvl in range(1, 6):
            for h in range(NH):
                pb = PB()
                x, xn = cur[h], oth[h]
                P.op("pe", lambda e, pb=pb, x=x: e.matmul(pb[:, 0:128], x[:, 128:256], x[:, 0:128], start=True, stop=True), reads=[x], writes=[pb])
                P.op("pe", lambda e, pb=pb, x=x: e.matmul(pb[:, 128:256], x[:, 0:128], x[:, 128:256], start=True, stop=True), reads=[x], writes=[pb])
                P.op("act", lambda e, pb=pb, xn=xn: e.activation(xn[:, :], pb[:, :], AF.Copy), reads=[pb], writes=[xn])
                pc = PC()
                P.op("pe", lambda e, pc=pc, xn=xn, h=h: e.matmul(pc[:, :], xn[:, 128:256], Qb[h][0][:], start=True, stop=True), reads=[xn, Qb[h][0]], writes=[pc])
                P.op("dve", lambda e, pc=pc, h=h: e.tensor_tensor(Q32[h][0][:], Q32[h][0][:], pc[:, :], ALU.add), reads=[pc, Q32[h][0]], writes=[Q32[h][0]])
                P.op("pool", lambda e, h=h: e.tensor_copy(Qb[h][0][:], Q32[h][0][:]), reads=[Q32[h][0]], writes=[Qb[h][0]])
                cur[h], oth[h] = xn, x
        if SW_STOP < 4:
            continue
        for h in range(NH):
            dk = dks[h]
            if h < 4:
                pc = PC()
                P.op("pe", lambda e, pc=pc, h=h: e.matmul(pc[:, 0:dk], M_Akk[h][par][:], Vtm[h][par][:, 0:dk], start=True, stop=True), reads=[M_Akk[h][par], Vtm[h][par]], writes=[pc])
                P.op("act", lambda e, pc=pc, h=h: e.activation(AVb[h][0][:, 0:dk], pc[:, 0:dk], AF.Copy), reads=[pc], writes=[AVb[h][0]])
                rhs_t = AVb[h][0]
            else:
                rhs_t = Vtm[h][par]
            pc = PC()
            P.op("pe", lambda e, pc=pc, h=h, rhs_t=rhs_t: e.matmul(pc[:, 0:dk], Qb[h][0][:], rhs_t[:, 0:dk], start=True, stop=True), reads=[Qb[h][0], rhs_t], writes=[pc])
            P.op("act", lambda e, pc=pc, h=h: e.activation(U0[h][par][:, 0:dk], pc[:, 0:dk], AF.Copy), reads=[pc], writes=[U0[h][par]])
            pc = PC()
            P.op("pe", lambda e, pc=pc, h=h: e.matmul(pc[0:dk, :], KAPtm[h][par][:, 0:dk], Qb[h][0][:], start=True, stop=True), reads=[KAPtm[h][par], Qb[h][0]], writes=[pc])
            P.op("act", lambda e, pc=pc, h=h: e.activation(WKT[h][par][0:dk, :], pc[0:dk, :], AF.Copy), reads=[pc], writes=[WKT[h][par]])
        if SW_STOP < 5:
            continue
        subs = [0, 1] if d == 0 else [1, 0]
        for sj in subs:
            r0 = 64 * sj
            for h in range(NH):
                dk = dks[h]
                pc = PC()
                P.op("pe", lambda e, pc=pc, h=h: e.matmul(pc[:, 0:dk], WKT[h][par][0:dk, :], Hbf[h][:, :], start=True, stop=True), reads=[WKT[h][par], Hbf[h]], writes=[pc])
                P.op("dve", lambda e, pc=pc, h=h: e.tensor_tensor(Usb[h][0][r0:r0 + 64, 0:dk], pc[r0:r0 + 64, 0:dk], U0[h][par][r0:r0 + 64, 0:dk], ALU.add),
                     reads=[pc, U0[h][par]], writes=[Usb[h][0]])
                SW_CH = int(os.environ.get("SW_CH", "9"))
                if SW_CH < 1:
                    continue
                if not isctx and SW_CH != 2:
                    py = PB()
                    pya = PC()
                    rtt = QA[h][par] if h < 4 else QG[h][par]
                    rta = rtt[0:dk, 128:256] if h < 4 else rtt[:, 0:128]
                    P.op("pe", lambda e, pya=pya, h=h, rta=rta: e.matmul(pya[:, 0:dk], rta, Hbf[h][:, :], start=True, stop=True), reads=[rtt, Hbf[h]], writes=[pya])
                    SW_Y = os.environ.get("SW_Y", "abc")
                    if h < 4 and "b" in SW_Y:
                        P.op("pe", lambda e, py=py, h=h: e.matmul(py[:, 128:128 + dk], M_Ark[h][par][r0:r0 + 64, :], Vtm[h][par][r0:r0 + 64, 0:dk], start=True, stop=("c" not in SW_Y)),
                             reads=[M_Ark[h][par], Vtm[h][par]], writes=[py])
                    if "c" in SW_Y:
                        P.op("pe", lambda e, py=py, h=h: e.matmul(py[:, 128:128 + dk], M_Arb[h][par][r0:r0 + 64, :], Usb[h][0][r0:r0 + 64, 0:dk], start=(h >= 4 or "b" not in SW_Y), stop=True),
                             reads=[M_Arb[h][par], Usb[h][0]], writes=[py])
                    P.op("act", lambda e, pya=pya, h=h: e.activation(Yt[h][par][r0:r0 + 64, 0:dk], pya[r0:r0 + 64, 0:dk], AF.Copy), reads=[pya], writes=[Yt[h][par]])
                    P.op("dve", lambda e, py=py, h=h: e.tensor_tensor(Yt[h][par][r0:r0 + 64, 0:dk], Yt[h][par][r0:r0 + 64, 0:dk], py[r0:r0 + 64, 128:128 + dk], ALU.add),
                         reads=[py, Yt[h][par]], writes=[Yt[h][par]])
                if SW_CH < 2:
                    continue
                ph = PC()
                if h < 4:
                    P.op("pe", lambda e, ph=ph, h=h: e.matmul(ph[0:dk, 0:dk], KEND[h][par][r0:r0 + 64, 0:dk], Vtm[h][par][r0:r0 + 64, 0:dk], start=True, stop=False),
                         reads=[KEND[h][par], Vtm[h][par]], writes=[ph])
                P.op("pe", lambda e, ph=ph, h=h: e.matmul(ph[0:dk, 0:dk], BEND[h][par][r0:r0 + 64, 0:dk], Usb[h][0][r0:r0 + 64, 0:dk], start=(h >= 4), stop=True),
                     reads=[BEND[h][par], Usb[h][0]], writes=[ph])
                P.op("dve", lambda e, ph=ph, h=h: e.scalar_tensor_tensor(H32[h][:, :], H32[h][:, :], GAM[h][par][0:dk, sj:sj + 1], ph[0:dk, 0:dk], ALU.mult, ALU.add),
                     reads=[ph, H32[h], GAM[h][par]], writes=[H32[h]])
                P.op("pool", lambda e, h=h: e.tensor_copy(Hbf[h][:, :], H32[h][:, :]), reads=[H32[h]], writes=[Hbf[h]])
        if not isctx and int(os.environ.get("SW_CH", "9")) not in (0, 2, 3):
            for h in range(NH):
                if h < 4:
                    P.dma("sp", Y['yr'].h[t0:t0 + 128, h * 64:(h + 1) * 64], Yt[h][par][:, 0:64], reads=[Yt[h][par]], writes=[Y['yr']])
                else:
                    P.dma("sp", Y['yg'].h[t0:t0 + 128, (h - 4) * 128:(h - 3) * 128], Yt[h][par][:, :], reads=[Yt[h][par]], writes=[Y['yg']])


def readout(P, k, I, pfm, S, Yd, out_a, out_b):
    lnw = _ld(P, I, 'ln_w', [128, 256]); lnb = _ld(P, I, 'ln_b', [128, 256]); gnw = _ld(P, I, 'gnw_row', [128, 256])
    y0 = [P.sb("y0", [128, 256]) for _ in range(2)]; y1 = [P.sb("y1", [128, 256]) for _ in range(2)]
    g0 = [P.sb("g0", [128, 256]) for _ in range(2)]; g1_ = [P.sb("g1_", [128, 256]) for _ in range(2)]
    vt = [P.sb("vt", [128, 256], BF16) for _ in range(2)]; gate = [P.sb("gate", [128, 256]) for _ in range(2)]
    bon = [P.sb("bon", [128, 4]) for _ in range(2)]
    zr = [P.sb("zr", [128, 128]) for _ in range(2)]
    zt = P.sb("zt", [128, 256]); st = P.sb("st", [128, 8]); junk = P.sb("junk", [128, 128])
    oa = [P.sb("oa", [128, 256]) for _ in range(2)]; ob = [P.sb("ob", [128, 256]) for _ in range(2)]
    pz = [P.ps("pz", [128, 128]) for _ in range(2)]
    for p_ in range(2):
        P.share([y0[p_], y1[p_], g0[p_], g1_[p_], vt[p_], gate[p_], bon[p_], zr[p_]], "rol")
        P.share([oa[p_], ob[p_]], "roo")
    P.share([lnw, lnb, gnw], "roc")
    for bi in range(TSEQ // 128):
        if bi % 32 == 31:
            P.barrier()
        t0 = 256 + bi * 128
        p = bi % 2
        P.dma("sp", y0[p][:], Yd[0]['yr'].h[t0:t0 + 128, :], writes=[y0[p]]); P.dma("sp", y1[p][:], Yd[1]['yr'].h[t0:t0 + 128, :], writes=[y1[p]])
        P.dma("sp", g0[p][:], Yd[0]['yg'].h[t0:t0 + 128, :], writes=[g0[p]]); P.dma("sp", g1_[p][:], Yd[1]['yg'].h[t0:t0 + 128, :], writes=[g1_[p]])
        P.dma("sp", vt[p][:], S['v_tm'].h[t0:t0 + 128, :], writes=[vt[p]]); P.dma("sp", gate[p][:], S['gate_tm'].h[t0:t0 + 128, :], writes=[gate[p]])
        P.dma("sp", bon[p][:], S['bonus_tm'].h[t0:t0 + 128, :], writes=[bon[p]])
        y = y0[p]
        P.op("dve", lambda e, y=y, p=p: e.tensor_tensor(y[:], y[:], y1[p][:], ALU.add), reads=[y, y1[p]], writes=[y])
        for h in range(4):
            ys = y[:, h * 64:(h + 1) * 64]
            P.op("dve", lambda e, ys=ys, h=h: e.tensor_reduce(st[:, h:h + 1], ys, AX.X, ALU.add), reads=[y], writes=[st])
        P.op("dve", lambda e: e.tensor_scalar(st[:, 0:4], st[:, 0:4], -1.0 / 64, None, ALU.mult), reads=[st], writes=[st])
        for h in range(4):
            ys = y[:, h * 64:(h + 1) * 64]
            P.op("dve", lambda e, ys=ys, h=h: e.tensor_scalar(ys, ys, st[:, h:h + 1], None, ALU.add), reads=[y, st], writes=[y])
            P.op("act", lambda e, ys=ys, h=h: e.activation(junk[:, 0:64], ys, AF.Square, accum_out=st[:, 4 + h:5 + h]), reads=[y], writes=[junk, st])
        P.op("act", lambda e: e.activation(st[:, 4:8], st[:, 4:8], AF.Sqrt, bias=k.gneps[:], scale=1.0 / 64), reads=[st, k.gneps], writes=[st])
        P.op("dve", lambda e: e.reciprocal(st[:, 4:8], st[:, 4:8]), reads=[st], writes=[st])
        for h in range(4):
            ys = y[:, h * 64:(h + 1) * 64]
            P.op("dve", lambda e, ys=ys, h=h: e.tensor_scalar(ys, ys, st[:, 4 + h:5 + h], None, ALU.mult), reads=[y, st], writes=[y])
        P.op("dve", lambda e, y=y: e.tensor_tensor(y[:], y[:], lnw[:], ALU.mult), reads=[y, lnw], writes=[y])
        P.op("pool", lambda e, y=y: e.tensor_tensor(y[:], y[:], lnb[:], ALU.add), reads=[y, lnb], writes=[y])
        for h in range(4):
            ys = y[:, h * 64:(h + 1) * 64]
            P.op("dve", lambda e, ys=ys, h=h, p=p: e.scalar_tensor_tensor(ys, vt[p][:, h * 64:(h + 1) * 64], bon[p][:, h:h + 1], ys, ALU.mult, ALU.add),
                 reads=[vt[p], bon[p], y], writes=[y])
        P.op("dve", lambda e, y=y, p=p: e.tensor_tensor(oa[p][:], y[:], gate[p][:], ALU.mult), reads=[y, gate[p]], writes=[oa[p]])
        P.dma("sp", out_a.h[t0 - 256:t0 - 256 + 128, :], oa[p][:], reads=[oa[p]], writes=[out_a])
        g = g0[p]
        P.op("dve", lambda e, g=g, p=p: e.tensor_tensor(g[:], g[:], g1_[p][:], ALU.add), reads=[g, g1_[p]], writes=[g])
        for hh in range(2):
            gs = g[:, hh * 128:(hh + 1) * 128]
            P.op("act", lambda e, gs=gs, hh=hh: e.activation(junk[:, :], gs, AF.Square, accum_out=st[:, hh:hh + 1]), reads=[g], writes=[junk, st])
        P.op("act", lambda e: e.activation(st[:, 0:2], st[:, 0:2], AF.Sqrt, bias=k.epsc[:], scale=1.0 / 128), reads=[st, k.epsc], writes=[st])
        P.op("dve", lambda e: e.reciprocal(st[:, 0:2], st[:, 0:2]), reads=[st], writes=[st])
        for hh in range(2):
            gs = g[:, hh * 128:(hh + 1) * 128]
            P.op("dve", lambda e, gs=gs, hh=hh: e.tensor_scalar(gs, gs, st[:, hh:hh + 1], None, ALU.mult), reads=[g, st], writes=[g])
            P.dma("sp", zr[hh][:], pfm.h[(16 + hh) * 128:(17 + hh) * 128, t0:t0 + 128], reads=[pfm], writes=[zr[hh]])
            P.op("pe", lambda e, hh=hh: e.transpose(pz[hh][:], zr[hh][:], k.ident[:]), reads=[zr[hh], k.ident], writes=[pz[hh]])
            P.op("act", lambda e, hh=hh: e.activation(zt[:, hh * 128:(hh + 1) * 128], pz[hh][:], AF.Silu), reads=[pz[hh]], writes=[zt])
        P.op("dve", lambda e, g=g: e.tensor_tensor(g[:], g[:], gnw[:], ALU.mult), reads=[g, gnw], writes=[g])
        P.op("dve", lambda e, g=g, p=p: e.tensor_tensor(ob[p][:], g[:], zt[:], ALU.mult), reads=[g, zt], writes=[ob[p]])
        P.dma("sp", out_b.h[t0 - 256:t0 - 256 + 128, :], ob[p][:], reads=[ob[p]], writes=[out_b])


def build_launch1(nc, m, stages=4):
    I = declare_inputs(nc, m)
    with ExitStack() as st:
        P = Prog(nc, st)
        k = build_common(P, I, st)
        k.onec = P.sb("onec", [128, 1]); P.op("pool", lambda e: e.memset(k.onec[:], 1.0), writes=[k.onec])
        abtm = P.dram("abtm", [TT, 8], F32)
        k.gneps = P.sb("gneps", [128, 1]); P.op("pool", lambda e: e.memset(k.gneps[:], 64e-5), writes=[k.gneps])
        pfm = P.dram("pfm", [NCT * 128, TT], F32)
        S = rwkv_scratch(P)
        Yd = [dict(yr=P.dram(f"yr{d}", [TT, 256], F32), yg=P.dram(f"yg{d}", [TT, 256], F32)) for d in range(2)]
        out_a = P.dram("ya", [TSEQ, 256], F32, kind="ExternalOutput")
        out_b = P.dram("yb", [TSEQ, 256], F32, kind="ExternalOutput")
        with P.scope():
            phase1(P, k, I, pfm, abtm)
        if stages >= 2:
            with P.scope():
                rwkv_prep(P, k, I, pfm, S)
        if stages >= 3:
            for d in range(2):
                with P.scope():
                    sweep(P, k, I, pfm, S, Yd[d], d, abtm)
        if stages >= 4:
            with P.scope():
                readout(P, k, I, pfm, S, Yd, out_a, out_b)
        P.barrier()
        P.emit()
        return P


NT4 = 4096
NB4 = 256


def p4_inputs(inp, b, j, ya_full, yb_full):
    f = np.float32
    m = {}
    rs = slice(NT4 * j, NT4 * (j + 1))
    m['xs'] = np.ascontiguousarray(inp['x'][b][rs])
    m['yas'] = np.ascontiguousarray(ya_full[rs]); m['ybs'] = np.ascontiguousarray(yb_full[rs])
    inw = inp['in_w'][0]
    m['wg'] = np.ascontiguousarray(inw[:, 7680:11776])
    m['up_a'] = inp['up_a'][0]; m['up_b'] = inp['up_b'][0]; m['out_w'] = inp['out_w'][0]
    m['w1'] = inp['mlp_w1'][0]; m['w2'] = inp['mlp_w2'][0]
    cc = np.stack([inp['c'][b], inp['c_ctx']], axis=1)
    m['cT'] = np.ascontiguousarray(cc.reshape(KC, 128, 2).transpose(1, 0, 2))
    m['mod_w'] = inp['mod_w'][0]
    m['mod_bT'] = np.ascontiguousarray(inp['mod_b'][0].reshape(96, 128).T)
    colT = lambda v: np.ascontiguousarray(v.reshape(-1, 128).T)
    m['n1T'] = colT(inp['norm1_w'][0]); m['n2T'] = colT(inp['norm2_w'][0]); m['nfT'] = colT(inp['final_norm_w'])
    m['ident'] = np.eye(128, dtype=f)
    return m


def phase4(P, k, I, out):
    N = NB4
    pools = dict(
        xr=[P.sb("xr", [128, D]) for _ in range(2)],
        tps=[P.ps("tps", [128, 512]) for _ in range(2)],
        sq=[P.sb("sq", [128, 512], BF16) for _ in range(2)],
        ssps=P.ps("ssps", [128, 512]),
        rstd=P.sb("rstd", [128, 512]),
        tmp=[P.sb("tmp", [128, 512]) for _ in range(2)],
    )
    P.share(pools['xr'], "xr4")
    xT = P.sb("xT", [128, KC, N]); hT = P.sb("hT", [128, KC, N], BF16); oT = P.sb("oT", [128, KC, N])
    yaT = P.sb("yaT", [128, 8, N], BF16); ybT = P.sb("ybT", [128, 8, N], BF16)
    mT = P.sb("mT", [128, KC, N], BF16); hid = P.sb("hid", [128, 64, N], BF16)
    gBt = P.sb("gBt", [128, KC, N])
    wb = [P.sb("wb", [128, 8192], BF16) for _ in range(3)]
    yr = [P.sb("yr", [128, 1024]) for _ in range(2)]
    P.share(yr, "yr4")
    pq = [P.ps("pq", [128, 512]) for _ in range(4)]
    sA = P.sb("sA", [128, N]); sB = P.sb("sB", [128, N]); t1 = P.sb("t1", [128, N]); t2 = P.sb("t2", [128, N])
    orow = pools['xr']
    wcnt = [0]

    def load_w(Wd, K, c0, ncol):
        t = wb[wcnt[0] % 3]; wcnt[0] += 1
        nk = K // 128
        v = t.h[:, 0:nk * ncol].rearrange("p (a b) -> p a b", b=ncol)
        wv = Wd.rearrange("(c p) n -> p c n", p=128)
        for kc in range(nk):
            P.dma("pool", v[:, kc, :], wv[:, kc, c0:c0 + ncol], writes=[t], group=True)
        return t, v

    for blk in range(NT4 // N):
        if blk % 2 == 1:
            P.barrier()
        tok0 = blk * N
        load_xT(P, k, I['xs'], tok0, N, xT, pools)
        norm_fm(P, k, xT, N, k.s1, k.sh1, hT, pools)
        for (src, dstT) in ((I['yas'], yaT), (I['ybs'], ybT)):
            for j in range(2):
                P.dma("sp", yr[j][:], src[tok0 + j * 128: tok0 + (j + 1) * 128, :], writes=[yr[j]])
            for c in range(8):
                pt = pools['tps'][c % 2]
                for j in range(2):
                    P.op("pe", lambda e, c=c, j=j, pt=pt: e.transpose(pt[:, j * 128:(j + 1) * 128], yr[j][:, c * 128:(c + 1) * 128], k.ident[:]),
                         reads=[yr[j], k.ident], writes=[pt])
                P.op("act", lambda e, c=c, pt=pt, dstT=dstT: e.activation(dstT[:, c, :], pt[:, 0:N], AF.Copy), reads=[pt], writes=[dstT])
        for cg in range(4):
            tga, vga = load_w(I['wg'], D, cg * 512, 512)
            tgb, vgb = load_w(I['wg'], D, 2048 + cg * 512, 512)
            tua, vua = load_w(I['up_a'], 1024, cg * 512, 512)
            for ct in range(4):
                ctg = cg * 4 + ct
                cs = slice(ct * 128, (ct + 1) * 128)
                for kc in range(KC):
                    P.op("pe", lambda e, kc=kc, cs=cs, vga=vga: e.matmul(pq[0][:, 0:N], vga[:, kc, cs], hT[:, kc, :], start=(kc == 0), stop=(kc == KC - 1)),
                         reads=[tga, hT], writes=[pq[0]])
                for kc in range(KC):
                    P.op("pe", lambda e, kc=kc, cs=cs, vgb=vgb: e.matmul(pq[1][:, 0:N], vgb[:, kc, cs], hT[:, kc, :], start=(kc == 0), stop=(kc == KC - 1)),
                         reads=[tgb, hT], writes=[pq[1]])
                for kc in range(8):
                    P.op("pe", lambda e, kc=kc, cs=cs, vua=vua: e.matmul(pq[2][:, 0:N], vua[:, kc, cs], yaT[:, kc, :], start=(kc == 0), stop=(kc == 7)),
                         reads=[tua, yaT], writes=[pq[2]])
                P.op("act", lambda e: e.activation(sA[:], pq[0][:, 0:N], AF.Sigmoid), reads=[pq[0]], writes=[sA])
                P.op("act", lambda e: e.activation(sB[:], pq[1][:, 0:N], AF.Sigmoid), reads=[pq[1]], writes=[sB])
                P.op("dve", lambda e, ctg=ctg: e.tensor_tensor(t1[:], sA[:], pq[2][:, 0:N], ALU.mult), reads=[sA, pq[2]], writes=[t1])
                P.op("pool", lambda e, ctg=ctg: e.tensor_copy(oT[:, ctg, :], t1[:]), reads=[t1], writes=[oT])
                P.op("pool", lambda e, ctg=ctg: e.tensor_copy(gBt[:, ctg, :], sB[:]), reads=[sB], writes=[gBt])
        for cg in range(4):
            tub, vub = load_w(I['up_b'], 1024, cg * 512, 512)
            for ct in range(4):
                ctg = cg * 4 + ct
                cs = slice(ct * 128, (ct + 1) * 128)
                for kc in range(8):
                    P.op("pe", lambda e, kc=kc, cs=cs, vub=vub: e.matmul(pq[3][:, 0:N], vub[:, kc, cs], ybT[:, kc, :], start=(kc == 0), stop=(kc == 7)),
                         reads=[tub, ybT], writes=[pq[3]])
                P.op("dve", lambda e, ctg=ctg: e.tensor_tensor(t2[:], gBt[:, ctg, :], pq[3][:, 0:N], ALU.mult), reads=[gBt, pq[3]], writes=[t2])
                P.op("pool", lambda e, ctg=ctg: e.tensor_tensor(mT[:, ctg, :], oT[:, ctg, :], t2[:], ALU.add), reads=[oT, t2], writes=[mT])
        for cg in range(4):
            two, vwo = load_w(I['out_w'], D, cg * 512, 512)
            for ct in range(4):
                ctg = cg * 4 + ct
                cs = slice(ct * 128, (ct + 1) * 128)
                ps = pq[ctg % 4]
                for kc in range(KC):
                    P.op("pe", lambda e, kc=kc, cs=cs, vwo=vwo, ps=ps: e.matmul(ps[:, 0:N], vwo[:, kc, cs], mT[:, kc, :], start=(kc == 0), stop=(kc == KC - 1)),
                         reads=[two, mT], writes=[ps])
                P.op("dve", lambda e, ctg=ctg, ps=ps: e.scalar_tensor_tensor(xT[:, ctg, :], ps[:, 0:N], k.g1[:, ctg:ctg + 1], xT[:, ctg, :], ALU.mult, ALU.add),
                     reads=[ps, k.g1, xT], writes=[xT])
        norm_fm(P, k, xT, N, k.s2, k.sh2, hT, pools)
        for fg in range(16):
            tw1, vw1 = load_w(I['w1'], D, fg * 512, 512)
            for ft in range(4):
                fi = fg * 4 + ft
                cs = slice(ft * 128, (ft + 1) * 128)
                ps = pq[fi % 4]
                for kc in range(KC):
                    P.op("pe", lambda e, kc=kc, cs=cs, vw1=vw1, ps=ps: e.matmul(ps[:, 0:N], vw1[:, kc, cs], hT[:, kc, :], start=(kc == 0), stop=(kc == KC - 1)),
                         reads=[tw1, hT], writes=[ps])
                tt = t1 if fi % 2 == 0 else t2
                P.op("act", lambda e, ps=ps, tt=tt: e.activation(tt[:], ps[:, 0:N], AF.Relu), reads=[ps], writes=[tt])
                P.op("dve" if fi % 2 == 0 else "pool", lambda e, fi=fi, tt=tt: e.tensor_tensor(hid[:, fi, :], tt[:], tt[:], ALU.mult), reads=[tt], writes=[hid])
        for ct in range(KC):
            tw2, vw2 = load_w(I['w2'], 8192, ct * 128, 128)
            ps = pq[ct % 4]
            for fc in range(64):
                P.op("pe", lambda e, fc=fc, vw2=vw2, ps=ps: e.matmul(ps[:, 0:N], vw2[:, fc, :], hid[:, fc, :], start=(fc == 0), stop=(fc == 63)),
                     reads=[tw2, hid], writes=[ps])
            P.op("dve", lambda e, ct=ct, ps=ps: e.scalar_tensor_tensor(xT[:, ct, :], ps[:, 0:N], k.g2[:, ct:ct + 1], xT[:, ct, :], ALU.mult, ALU.add),
                 reads=[ps, k.g2, xT], writes=[xT])
        norm_fm(P, k, xT, N, k.nf, None, oT, pools)
        for j in range(2):
            orw = orow[j]
            for q4 in range(4):
                pt = pools['tps'][q4 % 2]
                for i in range(4):
                    kc = q4 * 4 + i
                    P.op("pe", lambda e, kc=kc, i=i, j=j, pt=pt: e.transpose(pt[:, i * 128:(i + 1) * 128], oT[:, kc, j * 128:(j + 1) * 128], k.ident[:]),
                         reads=[oT, k.ident], writes=[pt])
                if q4 % 2 == 0:
                    P.op("act", lambda e, q4=q4, pt=pt, orw=orw: e.activation(orw[:, q4 * 512:(q4 + 1) * 512], pt[:, :], AF.Copy), reads=[pt], writes=[orw])
                else:
                    P.op("dve", lambda e, q4=q4, pt=pt, orw=orw: e.tensor_copy(orw[:, q4 * 512:(q4 + 1) * 512], pt[:, :]), reads=[pt], writes=[orw])
            P.dma("sp", out.h[tok0 + j * 128: tok0 + (j + 1) * 128, :], orw[:], reads=[orw], writes=[out])


def build_launch2(nc, m):
    I = declare_inputs(nc, m)
    with ExitStack() as st:
        P = Prog(nc, st)
        k = build_common(P, I, st)
        out = P.dram("out", [NT4, D], F32, kind="ExternalOutput")
        with P.scope():
            phase4(P, k, I, out)
        P.barrier()
        P.emit()
        return P


def kernel(**inputs):
    from concourse.bass_utils import run_bass_kernel_spmd
    inp = {k_: np.asarray(v) for k_, v in inputs.items()}
    in_maps = []
    for core in range(8):
        b, g = core // 4, core % 4
        m = core_inputs(inp, b, g); rwkv_inputs(inp, g, m); gdn_inputs(inp, g, m)
        in_maps.append(m)
    nc1 = bass.Bass("TRN2", target_bir_lowering=False)
    build_launch1(nc1, in_maps[0])
    res1 = run_bass_kernel_spmd(nc1, in_maps, core_ids=list(range(8))).results
    del in_maps
    ya = [np.concatenate([res1[b * 4 + g]["ya"] for g in range(4)], axis=1) for b in range(2)]
    yb = [np.concatenate([res1[b * 4 + g]["yb"] for g in range(4)], axis=1) for b in range(2)]
    del res1
    in_maps2 = []
    for core in range(8):
        b, j = core // 4, core % 4
        in_maps2.append(p4_inputs(inp, b, j, ya[b], yb[b]))
    nc2 = bass.Bass("TRN2", target_bir_lowering=False)
    build_launch2(nc2, in_maps2[0])
    res2 = run_bass_kernel_spmd(nc2, in_maps2, core_ids=list(range(8))).results
    out = np.empty((2, TSEQ, D), np.float32)
    for core in range(8):
        b, j = core // 4, core % 4
        out[b, NT4 * j:NT4 * (j + 1)] = res2[core]["out"]
    return out
```
